# Optimizing a Trainium2 kernel written in Bass

```python
import jax, jax.numpy as jnp
from jax import lax
import numpy as np

D_MODEL = 1024
BATCH = 32
SEQ = 2048
DEPTH = 1

HEAD_DIM = 64
N_ATT_HEADS = 6
N_KV_GROUPS = 2
ATT_WIDTH = N_ATT_HEADS * HEAD_DIM
KV_WIDTH = N_KV_GROUPS * HEAD_DIM
N_IDX_HEADS = 8
IDX_DIM = 32
TOPK_MAX = 256
Q_BLOCK = 128
CONV_CHANNELS = 384
SHORT_CONV_K = 3
N_CROSS_HEADS = 4
CROSS_WIDTH = N_CROSS_HEADS * HEAD_DIM
MEM_LEN = 256
N_BRANCHES = 3
D_FF = 2816
FFN_CONV_K = 3
ROPE_THETA = 500000.0
ROPE_FRACTION = 4
NORM_EPS = 1e-6

PROJ_SIZES = (ATT_WIDTH, KV_WIDTH, KV_WIDTH, N_IDX_HEADS * IDX_DIM, IDX_DIM, N_IDX_HEADS,
              3 * CONV_CHANNELS, CROSS_WIDTH, N_BRANCHES * D_MODEL)
PROJ_WIDTH = sum(PROJ_SIZES)

kernel_name = "hybrid_dsa_shortconv_memxattn_convffn"


def rms_norm(x, g):
    xf = x.astype(jnp.float32)
    y = xf * lax.rsqrt(jnp.mean(xf * xf, axis=-1, keepdims=True) + NORM_EPS)
    return (y * g.astype(jnp.float32)).astype(x.dtype)


def split_cols(a, sizes):
    out, off = [], 0
    for n in sizes:
        out.append(a[..., off:off + n])
        off += n
    return out


def rope_tables(positions, rot_dim, dtype):
    half = rot_dim // 2
    inv_freq = ROPE_THETA ** (-jnp.arange(half, dtype=jnp.float32) / half)
    ang = positions.astype(jnp.float32)[:, None] * inv_freq[None, :]
    return jnp.cos(ang).astype(dtype), jnp.sin(ang).astype(dtype)


def partial_rope(x, cos, sin):
    half = cos.shape[-1]
    shape = (1, cos.shape[0]) + (1,) * (x.ndim - 3) + (half,)
    c, s = cos.reshape(shape), sin.reshape(shape)
    x1, x2, rest = x[..., :half], x[..., half:2 * half], x[..., 2 * half:]
    return jnp.concatenate([x1 * c - x2 * s, x2 * c + x1 * s, rest], axis=-1)


def causal_dwconv(u, w):
    k, c = w.shape
    return lax.conv_general_dilated(u, w[:, None, :].astype(u.dtype), window_strides=(1,),
                                    padding=[(k - 1, 0)],
                                    dimension_numbers=("NWC", "WIO", "NWC"),
                                    feature_group_count=c)


def dsa_attention(q, k, v, qi, ki, wi):
    b, s = q.shape[:2]
    topk = min(TOPK_MAX, s // 4)
    blk = min(Q_BLOCK, s)
    nb = s // blk
    rep = N_ATT_HEADS // N_KV_GROUPS
    idx_scale = (N_IDX_HEADS * IDX_DIM) ** -0.5
    att_scale = HEAD_DIM ** -0.5
    key_pos = jnp.arange(s)
    gather = jax.vmap(lambda a, i: a[i])

    def to_blocks(a):
        return jnp.moveaxis(a.reshape((b, nb, blk) + a.shape[2:]), 1, 0)

    def one_block(args):
        q_b, qi_b, wi_b, start = args
        qpos = start + jnp.arange(blk)
        causal = key_pos[None, :] <= qpos[:, None]
        rel = jax.nn.relu(jnp.einsum('bqhd,bsd->bqhs', qi_b, ki).astype(jnp.float32))
        score = jnp.einsum('bqh,bqhs->bqs', wi_b.astype(jnp.float32), rel) * idx_scale
        score = jnp.where(causal[None], score, -jnp.inf)
        _, sel = lax.top_k(score, topk)
        k_sel = gather(k, sel)
        v_sel = gather(v, sel)
        qg = q_b.reshape(b, blk, N_KV_GROUPS, rep, HEAD_DIM)
        logits = jnp.einsum('bqgrd,bqkgd->bqgrk', qg, k_sel).astype(jnp.float32) * att_scale
        valid = sel <= qpos[None, :, None]
        logits = jnp.where(valid[:, :, None, None, :], logits, -jnp.inf)
        p = jax.nn.softmax(logits, axis=-1).astype(v.dtype)
        o = jnp.einsum('bqgrk,bqkgd->bqgrd', p, v_sel)
        return o.reshape(b, blk, N_ATT_HEADS * HEAD_DIM)

    starts = jnp.arange(nb) * blk
    out = lax.map(one_block, (to_blocks(q), to_blocks(qi), to_blocks(wi), starts))
    return jnp.moveaxis(out, 0, 1).reshape(b, s, N_ATT_HEADS * HEAD_DIM)


def memory_attention(qc, km, vm):
    b, s = qc.shape[:2]
    logits = jnp.einsum('bshd,bmhd->bhsm', qc, km).astype(jnp.float32) * HEAD_DIM ** -0.5
    p = jax.nn.softmax(logits, axis=-1).astype(vm.dtype)
    return jnp.einsum('bhsm,bmhd->bshd', p, vm).reshape(b, s, CROSS_WIDTH)


def setup_inputs(seed: int = 0) -> dict:
    key = jax.random.key(seed)
    ks = jax.random.split(key, 20)

    def w(k, shape, fan_in):
        return jax.random.normal(k, shape, jnp.float32) * fan_in ** -0.5

    def gain(k, shape):
        return 1.0 + 0.02 * jax.random.normal(k, shape, jnp.float32)

    return {
        "x": jax.random.normal(ks[0], (BATCH, SEQ, D_MODEL), jnp.float32),
        "mem": jax.random.normal(ks[1], (BATCH, MEM_LEN, D_MODEL), jnp.float32),
        "g_mix": gain(ks[2], (DEPTH, D_MODEL)),
        "w_in": w(ks[3], (DEPTH, D_MODEL, PROJ_WIDTH), D_MODEL),
        "b_gate": 0.02 * jax.random.normal(ks[4], (DEPTH, N_BRANCHES * D_MODEL), jnp.float32),
        "conv_w_short": w(ks[5], (DEPTH, SHORT_CONV_K, CONV_CHANNELS), SHORT_CONV_K),
        "w_att_out": w(ks[6], (DEPTH, ATT_WIDTH, D_MODEL), ATT_WIDTH),
        "w_conv_out": w(ks[7], (DEPTH, CONV_CHANNELS, D_MODEL), CONV_CHANNELS),
        "w_mem_out": w(ks[8], (DEPTH, CROSS_WIDTH, D_MODEL), CROSS_WIDTH),
        "w_o": w(ks[9], (DEPTH, D_MODEL, D_MODEL), D_MODEL),
        "g_mem": gain(ks[10], (DEPTH, D_MODEL)),
        "w_mem_kv": w(ks[11], (DEPTH, D_MODEL, 2 * CROSS_WIDTH), D_MODEL),
        "g_ffn": gain(ks[12], (DEPTH, D_MODEL)),
        "w_up": w(ks[13], (DEPTH, D_MODEL, 2 * D_FF), D_MODEL),
        "conv_w_ffn": w(ks[14], (DEPTH, FFN_CONV_K, 2 * D_FF), FFN_CONV_K),
        "w_down": w(ks[15], (DEPTH, D_FF, D_MODEL), D_FF),
        "g_final": gain(ks[16], (D_MODEL,)),
    }


def reference(x, mem, g_mix, w_in, b_gate, conv_w_short, w_att_out, w_conv_out, w_mem_out,
              w_o, g_mem, w_mem_kv, g_ffn, w_up, conv_w_ffn, w_down, g_final):
    b, s, d = x.shape
    m = mem.shape[1]
    positions = jnp.arange(s)
    cos_a, sin_a = rope_tables(positions, HEAD_DIM // ROPE_FRACTION, x.dtype)
    cos_i, sin_i = rope_tables(positions, IDX_DIM // ROPE_FRACTION, x.dtype)

    for l in range(DEPTH):
        h = rms_norm(x, g_mix[l])
        proj = h @ w_in[l]
        q, k, v, qi, ki, wi, conv_in, qc, gate_pre = split_cols(proj, PROJ_SIZES)

        q = partial_rope(q.reshape(b, s, N_ATT_HEADS, HEAD_DIM), cos_a, sin_a)
        k = partial_rope(k.reshape(b, s, N_KV_GROUPS, HEAD_DIM), cos_a, sin_a)
        v = v.reshape(b, s, N_KV_GROUPS, HEAD_DIM)
        qi = partial_rope(qi.reshape(b, s, N_IDX_HEADS, IDX_DIM), cos_i, sin_i)
        ki = partial_rope(ki, cos_i, sin_i)
        y_att = dsa_attention(q, k, v, qi, ki, wi) @ w_att_out[l]

        bg, cg, u = jnp.split(conv_in, 3, axis=-1)
        y_conv = (bg * causal_dwconv(cg * u, conv_w_short[l])) @ w_conv_out[l]

        mem_kv = (rms_norm(mem, g_mem[l]) @ w_mem_kv[l]).reshape(b, m, 2, N_CROSS_HEADS, HEAD_DIM)
        y_mem = memory_attention(qc.reshape(b, s, N_CROSS_HEADS, HEAD_DIM),
                                 mem_kv[:, :, 0], mem_kv[:, :, 1]) @ w_mem_out[l]

        gates = jax.nn.sigmoid(gate_pre + b_gate[l]).reshape(b, s, N_BRANCHES, d)
        merged = gates[:, :, 0] * y_att + gates[:, :, 1] * y_conv + gates[:, :, 2] * y_mem
        x = x + merged @ w_o[l]

        h = rms_norm(x, g_ffn[l])
        up = causal_dwconv(h @ w_up[l], conv_w_ffn[l])
        gate, val = jnp.split(up, 2, axis=-1)
        x = x + (jax.nn.silu(gate) * val) @ w_down[l]

    return rms_norm(x, g_final)
```

```python
import numpy as np
import concourse.bass as bass
import concourse.mybir as mybir
from concourse.bass_utils import run_bass_kernel_spmd

F32 = mybir.dt.float32
BF16 = mybir.dt.bfloat16
AF = mybir.ActivationFunctionType
ALU = mybir.AluOpType
AX = mybir.AxisListType

D = 1024
NCORES = 8
TC = 512
MEM = 256
DFF = 2816
EPS = 1e-6
BIG = 1.0e30
KIT = 12
SAME_ENGINE_SKIP = ("pe",)
ROPE_THETA = 500000.0


class Buf:
    __slots__ = ("w", "r", "sem", "cnt", "name")

    def __init__(self, name=""):
        self.w = {}
        self.r = {}
        self.sem = None
        self.cnt = 0
        self.name = name


class V:
    def __init__(self, bufs, ap):
        self.bufs = bufs
        self.ap = ap

    def __getitem__(self, k):
        return V(self.bufs, self.ap[k])

    def re(self, pat, **kw):
        return V(self.bufs, self.ap.rearrange(pat, **kw))

    def bcast(self, shape):
        return V(self.bufs, self.ap.broadcast_to(shape))


class Eng:
    def __init__(self, name, sem):
        self.name = name
        self.sem = sem
        self.ops = []


class Ctx:
    def __init__(self, nc):
        self.nc = nc
        self.E = {}
        for n in ("pe", "act", "dve", "pool", "sp"):
            self.E[n] = Eng(n, nc.alloc_semaphore("sem_" + n))
        self.nsem = 5

    def _need(self, en, reads, writes, is_dma):
        need = {}

        def add(d, skip_own):
            for k, val in d.items():
                if skip_own and k == ("e", en):
                    continue
                if need.get(k, -1) < val:
                    need[k] = val

        for v in reads:
            for b in v.bufs:
                add(b.w, False)
        skip = (not is_dma) and en in SAME_ENGINE_SKIP
        for v in writes:
            for b in v.bufs:
                add(b.w, skip)
                add(b.r, skip)
        return need

    def op(self, en, fn, reads=(), writes=()):
        E = self.E[en]
        need = self._need(en, reads, writes, False)
        for k, val in need.items():
            if k[0] == "e":
                self.E[k[1]].ops[val][3] = True
        idx = len(E.ops)
        E.ops.append([need, fn, None, False])
        key = ("e", en)
        for v in reads:
            for b in v.bufs:
                b.r[key] = idx
        for v in writes:
            for b in v.bufs:
                b.w = {key: idx}
                b.r = {}

    def dma(self, q, out, in_, owner, reads=(), writes=(), merge_w=False):
        E = self.E[q]
        need = self._need(q, reads, () if merge_w else writes, True)
        for k, val in need.items():
            if k[0] == "e":
                self.E[k[1]].ops[val][3] = True
        if owner.sem is None:
            owner.sem = self.nc.alloc_semaphore("dsem%d" % self.nsem)
            self.nsem += 1
        owner.cnt += 16
        key = ("d", id(owner), owner)
        oap, iap = out.ap, in_.ap
        E.ops.append([need, lambda e: e.dma_start(out=oap, in_=iap), (owner.sem, 16), True])
        for v in reads:
            for b in v.bufs:
                b.r[key] = owner.cnt
        for v in writes:
            for b in v.bufs:
                if merge_w:
                    b.w[key] = owner.cnt
                else:
                    b.w = {key: owner.cnt}
                    b.r = {}

    def emit(self, final_waits):
        nc = self.nc
        num = {}
        for n, E in self.E.items():
            c = 0
            m = []
            for o in E.ops:
                if o[2] is None and o[3]:
                    c += 1
                m.append(c)
            num[n] = m
        plan = {"pe": "tensor", "act": "scalar", "dve": "vector", "pool": "gpsimd", "sp": "sync"}
        with nc.Block() as block:
            for n, attr in plan.items():
                E = self.E[n]

                def body(e, E=E, n=n):
                    waited = {}
                    def do_wait(k, val):
                        if k[0] == "e":
                            sem = self.E[k[1]].sem
                            v = num[k[1]][val]
                        else:
                            sem = k[2].sem
                            v = val
                        kk = id(sem)
                        if waited.get(kk, 0) >= v:
                            return
                        waited[kk] = v
                        e.wait_ge(sem, v)
                    for need, fn, inc, need_inc in E.ops:
                        for k, val in need.items():
                            do_wait(k, val)
                        ins = fn(e)
                        if inc is not None:
                            ins.then_inc(inc[0], inc[1])
                        elif need_inc:
                            ins.then_inc(E.sem, 1)
                    if n == "sp":
                        for b in final_waits:
                            for k, val in list(b.r.items()) + list(b.w.items()):
                                do_wait(k, val)

                getattr(block, attr)(body)


class RPool:
    def __init__(self, items):
        self.items = items
        self.i = 0

    def get(self):
        it = self.items[self.i % len(self.items)]
        self.i += 1
        return it


def build(S, NSEQ, dbg=None):
    assert S % TC == 0
    NCH = S // TC
    NT = S // 128
    TOPK = min(256, S // 4)
    nc = bass.Bass("TRN2", target_bir_lowering=False)
    C = Ctx(nc)

    def dram(name, shape, dt=F32, kind="ExternalInput"):
        return V([Buf(name)], nc.dram_tensor(name, shape, dt, kind=kind).ap())

    x_d = dram("x", [NSEQ, S, D])
    mem_d = dram("mem", [NSEQ, MEM, D])
    out_d = dram("out", [NSEQ, S, D], kind="ExternalOutput")
    wspec = [("wA", 1024, 3456), ("wV", 1024, 136), ("wG", 1024, 3072), ("wKV", 1024, 512),
             ("wAO", 384, 1024), ("wCO", 384, 1024), ("wMO", 256, 1024), ("wO", 1024, 1024),
             ("wU", 1024, 5632), ("wD", 2816, 1024)]
    wf, ws = {}, {}
    for n, k, m in wspec:
        wf[n] = dram(n, [k, m])
        ws[n] = dram("s" + n, [k, m], BF16, kind="Internal")
    gcol_d = dram("gcol", [128, 24])
    gfin_d = dram("gfin", [128, D])
    bgate_d = dram("bgate", [128, 24])
    cws_d = dram("cws", [128, 9])
    cwf_d = dram("cwf", [128, 132])
    rope_d = dram("rope", [4, 128, S])
    pw2_d = dram("pw2", [128, KIT])

    def sb(name, shape, dt, nbuf=None):
        t = nc.alloc_sbuf_tensor(name, shape, dt)
        return V([Buf(name)], t[:]), t

    ident, _ = sb("ident", [128, 128], BF16)
    gcol, _ = sb("gcol_s", [128, 24], F32)
    gfin, _ = sb("gfin_s", [128, D], F32)
    bgate, _ = sb("bgate_s", [128, 24], F32)
    cws, _ = sb("cws_s", [128, 9], F32)
    cwf, _ = sb("cwf_s", [128, 132], F32)
    pw2, _ = sb("pw2_s", [128, KIT], F32)
    ropeT, _ = sb("ropeT", [128, 4, TC], F32)
    wV, _ = sb("wV_s", [128, 8, 136], BF16)
    wAO, _ = sb("wAO_s", [128, 3, D], BF16)
    wCO, _ = sb("wCO_s", [128, 3, D], BF16)
    wMO, _ = sb("wMO_s", [128, 2, D], BF16)
    xpool = RPool([sb("xt%d" % i, [128, D], F32)[0] for i in range(2)])
    x1, _ = sb("x1", [128, 4, D], F32)
    hbs = RPool([sb("hb%d" % i, [128, D], BF16)[0] for i in range(2)])
    hb = hbs.items[0]
    hT, _ = sb("hT", [128, 8, TC], BF16)
    h2T, _ = sb("h2T", [128, 8, TC], BF16)
    _, qm_t = sb("qm", [128, 4096], BF16)
    bq, bqi, bqc, bqx = Buf("qT"), Buf("qiT"), Buf("qcT"), Buf("qx")
    qT = V([bq], qm_t[:, 0:1536]).re("p (k t) -> p k t", k=3)
    qiT = V([bqi], qm_t[:, 1536:3072]).re("p (k t) -> p k t", k=3)
    qcT = V([bqc], qm_t[:, 3072:4096]).re("p (k t) -> p k t", k=2)
    mergedT = V([bq, bqi, bqc, bqx], qm_t[:, 0:4096]).re("p (k t) -> p k t", k=8)
    convT, _ = sb("convT", [128, 3, TC], BF16)
    attT, _ = sb("attT", [128, 3, TC], BF16)
    memattT, _ = sb("memattT", [128, 2, TC], BF16)
    kT, _ = sb("kT", [128, S], BF16)
    kiT, _ = sb("kiT", [128, S], BF16)
    vtok, _ = sb("vtok", [128, NT, 2, 65], BF16)
    kmT, _ = sb("kmT", [128, 2, MEM], BF16)
    vmtok, _ = sb("vmtok", [128, 2, 4, 65], BF16)
    wi_sb, _ = sb("wi_sb", [128, 4, 8], F32)
    sc_t = nc.alloc_sbuf_tensor("scact", [128, 16384], BF16)
    sc_f = sc_t.bitcast(F32)
    scb = [Buf("sc%d" % i) for i in range(4)]
    scores = [V([scb[i]], sc_f[:, i * 2048:(i + 1) * 2048]) for i in range(4)]
    actT = V(scb, sc_t[:, 0:22 * TC]).re("p (k t) -> p k t", k=22)
    Mb, Mb_t = sb("Mb", [128, 2048], BF16)
    MT, MT_t = sb("MT", [128, 16, 128], BF16)
    ub_items = []
    for i in range(4):
        t_ = nc.alloc_sbuf_tensor("ub%d" % i, [128, 516], F32)
        bh_, bm_ = Buf("ubh%d" % i), Buf("ubm%d" % i)
        ub_items.append((V([bh_, bm_], t_[:]), V([bh_], t_[:, 0:2]), V([bm_], t_[:, 2:2 + TC])))
    tmp = RPool([sb("tmp%d" % i, [128, 516], F32)[0] for i in range(8)])
    Mb_f = Mb_t.bitcast(F32)
    MT_f = MT_t.bitcast(F32)[:].rearrange("p a b -> p (a b)")
    alias_acc = [V(Mb.bufs, Mb_f[:, 0:512]), V(Mb.bufs, Mb_f[:, 512:1024]),
                 V(MT.bufs, MT_f[:, 0:512]), V(MT.bufs, MT_f[:, 512:1024])]
    ffn_ub = RPool(ub_items)
    ffn_acc = RPool([t_[:, 0:512] for t_ in tmp.items] + alias_acc)
    ebuf = RPool([sb("eb%d" % i, [128, 512], BF16)[0] for i in range(6)])
    att_tok, _ = sb("att_tok", [128, 384], BF16)
    mem_tok, _ = sb("mem_tok", [128, 256], BF16)
    chalo, _ = sb("chalo", [128, 3, 2], F32)
    fhalo, _ = sb("fhalo", [128, 44, 2], F32)
    obuf = RPool([sb("ob%d" % i, [128, D], F32)[0] for i in range(1)])
    wbig = RPool([sb("wb%d" % i, [128, 4096], BF16)[0] for i in range(3)])
    _, sm_t = sb("smalls", [128, 160], F32)
    smi = [0]

    def small(n=1):
        i = smi[0]
        smi[0] += n
        assert smi[0] <= 160
        return V([Buf("sm%d" % i)], sm_t[:, i:i + n])

    rms_s = RPool([[small(4), small(4), small(4), small(4)] for _ in range(3)])
    ch_s = [dict(mx=small(), mn=small(), nr=small(), nst=small(KIT), nm=small(), S=small(), pm=small(), t=small())
            for _ in range(4)]
    rden_a = RPool([small(8) for _ in range(2)])

    psl = []
    for i in range(8):
        t = nc.alloc_psum_tensor("ps%d" % i, [128, 512], F32)
        b = Buf("ps%d" % i)
        psl.append((V([b], t[:]), V([b], t.bitcast(BF16)[:])))
    psum = RPool(psl[0:4])
    psB = RPool(psl[4:8])
    psum8 = RPool(psl)
    print("sbuf bytes remaining:", nc.sbuf_bytes_remaining)

    def mm(out, lhsT, rhs, start, stop):
        o, l, r = out.ap, lhsT.ap, rhs.ap
        C.op("pe", lambda e: e.matmul(o, lhsT=l, rhs=r, start=start, stop=stop), [lhsT, rhs], [out])

    def tr(out, in_):
        o, i, idn = out.ap, in_.ap, ident.ap
        C.op("pe", lambda e: e.transpose(out=o, in_=i, identity=idn), [in_, ident], [out])

    def act(out, in_, func, bias=None, scale=None, accum=None):
        o, i = out.ap, in_.ap
        kw = {}
        rd, wr = [in_], [out]
        if bias is not None:
            if isinstance(bias, V):
                kw["bias"] = bias.ap
                rd.append(bias)
            else:
                kw["bias"] = float(bias)
        if scale is not None:
            if isinstance(scale, V):
                kw["scale"] = scale.ap
                rd.append(scale)
            else:
                kw["scale"] = float(scale)
        if accum is not None:
            kw["accum_out"] = accum.ap
            wr.append(accum)
        C.op("act", lambda e: e.activation(out=o, in_=i, func=func, **kw), rd, wr)

    def tt(en, out, a, b, op):
        o, x, y = out.ap, a.ap, b.ap
        C.op(en, lambda e: e.tensor_tensor(out=o, in0=x, in1=y, op=op), [a, b], [out])

    def ts(en, out, a, s1, s2, op0, op1=None, accum=None):
        o, x = out.ap, a.ap
        rd, wr = [a], [out]
        if isinstance(s1, V):
            rd.append(s1)
            s1 = s1.ap
        if isinstance(s2, V):
            rd.append(s2)
            s2 = s2.ap
        kw = {}
        if op1 is not None:
            kw["op1"] = op1
        if accum is not None:
            kw["accum_out"] = accum.ap
            wr.append(accum)
        C.op(en, lambda e: e.tensor_scalar(out=o, in0=x, scalar1=s1, scalar2=s2, op0=op0, **kw), rd, wr)

    def stt(out, a, s, b, op0, op1):
        o, x, y = out.ap, a.ap, b.ap
        rd = [a, b]
        if isinstance(s, V):
            rd.append(s)
            s = s.ap
        C.op("dve", lambda e: e.scalar_tensor_tensor(out=o, in0=x, scalar=s, in1=y, op0=op0, op1=op1), rd, [out])

    def cp(en, out, in_):
        if en == "act":
            return act(out, in_, AF.Copy)
        o, i = out.ap, in_.ap
        C.op(en, lambda e: e.tensor_copy(out=o, in_=i), [in_], [out])

    def memset(en, out, val):
        o = out.ap
        C.op(en, lambda e: e.memset(o, val), [], [out])

    def recip(out, in_):
        o, i = out.ap, in_.ap
        C.op("dve", lambda e: e.reciprocal(out=o, in_=i), [in_], [out])

    def reduce(out, in_, op):
        o, i = out.ap, in_.ap
        C.op("dve", lambda e: e.tensor_reduce(out=o, in_=i, axis=AX.X, op=op), [in_], [out])

    def load(dst, src, owner=None):
        C.dma("sp", dst, src, owner or dst.bufs[0], reads=[src], writes=[dst])

    def load_w(src3, KB, n):
        slot = wbig.get()
        view = slot[:, 0:KB * n].re("p (k n) -> p k n", k=KB)
        load(view, src3)
        return view

    regcache = {}

    def fillreg(e, val):
        if val not in regcache:
            regcache[val] = e.to_reg(val)
        return regcache[val]

    dbg_out = {}

    def dump(name, v, shape, dt=F32):
        if dbg is None or name not in dbg:
            return
        d = dram("dbg_" + name, shape, dt, kind="ExternalOutput")
        dbg_out[name] = d
        C.dma("sp", d, v, v.bufs[0], reads=[v], writes=[d])

    memset("pool", ident, 0.0)
    ia = ident.ap
    C.op("pool", lambda e: e.affine_select(out=ia, in_=ia, pattern=[[-1, 128]], compare_op=ALU.not_equal,
                                           fill=1.0, base=0, channel_multiplier=1), [ident], [ident])
    for dst, src in ((gcol, gcol_d), (gfin, gfin_d), (bgate, bgate_d), (cws, cws_d), (cwf, cwf_d), (pw2, pw2_d)):
        load(dst, src)
    memset("pool", vtok, 1.0)
    memset("pool", vmtok, 1.0)

    stg_f = RPool(scores)
    stg_b = RPool([Mb, MT.re("p a b -> p (a b)")])
    tiles = []
    for n, k, m in wspec:
        for r0 in range(0, k, 128):
            for c0 in range(0, m, 2048):
                tiles.append((n, r0, c0, min(2048, m - c0)))
    fbufs = {}

    def cv_load(ix):
        n, r0, c0, w = tiles[ix]
        f = stg_f.get()
        fbufs[ix] = f
        load(f[:, 0:w], wf[n][r0:r0 + 128, c0:c0 + w])

    for ix in range(min(3, len(tiles))):
        cv_load(ix)
    for ix, (n, r0, c0, w) in enumerate(tiles):
        f = fbufs.pop(ix)
        b = stg_b.get()
        cp("dve" if ix % 2 == 0 else "pool", b[:, 0:w], f[:, 0:w])
        C.dma("act", ws[n][r0:r0 + 128, c0:c0 + w], b[:, 0:w], b.bufs[0], reads=[b], writes=[ws[n]], merge_w=True)
        if ix + 3 < len(tiles):
            cv_load(ix + 3)
    load(wV, ws["wV"].re("(kb p) n -> p kb n", p=128))
    load(wAO, ws["wAO"].re("(kb p) n -> p kb n", p=128))
    load(wCO, ws["wCO"].re("(kb p) n -> p kb n", p=128))
    load(wMO, ws["wMO"].re("(kb p) n -> p kb n", p=128))
    sA3 = ws["wA"].re("(kb p) n -> p kb n", p=128)
    sG3 = ws["wG"].re("(kb p) n -> p kb n", p=128)
    sKV3 = ws["wKV"].re("(kb p) n -> p kb n", p=128)
    sO3 = ws["wO"].re("(kb p) n -> p kb n", p=128)
    sU3 = ws["wU"].re("(kb p) n -> p kb n", p=128)
    sD3 = ws["wD"].re("(kb p) n -> p kb n", p=128)

    evac_rr = [0]

    def rms_stats(xvs):
        ss, var, sd, rstd = rms_s.get()
        n = len(xvs)
        for i, xv in enumerate(xvs):
            act(hbs.items[i % 2], xv, AF.Square, accum=ss[:, i:i + 1])
        ts("dve", var[:, 0:n], ss[:, 0:n], 1.0 / D, EPS, ALU.mult, ALU.add)
        act(sd[:, 0:n], var[:, 0:n], AF.Sqrt)
        recip(rstd[:, 0:n], sd[:, 0:n])
        return rstd

    def rms_to_T_multi(xvs, gc, dstT, tixs):
        rstd = rms_stats(xvs)
        for i, (xv, tix) in enumerate(zip(xvs, tixs)):
            h_ = hbs.get()
            act(h_, xv, AF.Copy, scale=rstd[:, i:i + 1])
            pf, pb = psum.get()
            for kb in range(8):
                tr(pb[:, kb * 128:(kb + 1) * 128], h_[:, kb * 128:(kb + 1) * 128])
            tt("dve", dstT[:, :, tix * 128:(tix + 1) * 128], pb.re("p (k t) -> p k t", k=8),
               gc.re("p (k o) -> p k o", o=1).bcast([128, 8, 128]), ALU.mult)

    gmix_c, gmem_c, gffn_c = gcol[:, 0:8], gcol[:, 8:16], gcol[:, 16:24]

    for seq in range(NSEQ):
        memhT = h2T
        xts = []
        for mt in range(2):
            xt = xpool.get()
            load(xt, mem_d[seq, mt * 128:(mt + 1) * 128, :])
            xts.append(xt)
        rms_to_T_multi(xts, gmem_c, memhT, [0, 1])
        wkv = load_w(sKV3, 8, 512)
        for blk in range(2):
            pf, pb = psum.get()
            for kb in range(8):
                mm(pf[:, 0:MEM], wkv[:, kb, blk * 128:(blk + 1) * 128], memhT[:, kb, 0:MEM], kb == 0, kb == 7)
            cp("act", kmT[:, blk, :], pf[:, 0:MEM])
        for mt in range(2):
            pf, pb = psum.get()
            for kb in range(8):
                mm(pf[:, 0:256], memhT[:, kb, mt * 128:(mt + 1) * 128], wkv[:, kb, 256:512], kb == 0, kb == 7)
            cp("act", vmtok[:, mt, :, 0:64], pf[:, 0:256].re("p (h d) -> p h d", h=4))
        memset("pool", chalo, 0.0)
        memset("pool", fhalo, 0.0)

        def stage1(j):
            tok0 = j * TC
            load(ropeT, rope_d[:, :, tok0:tok0 + TC].re("f p t -> p f t"))
            for half in range(2):
                xts = []
                for tix in (2 * half, 2 * half + 1):
                    xt = xpool.get()
                    load(xt, x_d[seq, tok0 + tix * 128: tok0 + (tix + 1) * 128, :])
                    xts.append(xt)
                rms_to_T_multi(xts, gmix_c, hT, [2 * half, 2 * half + 1])

        stage1(0)
        for j in range(NCH):
            tok0 = j * TC
            cos64, sin64, cos32, sin32 = ropeT[:, 0, :], ropeT[:, 1, :], ropeT[:, 2, :], ropeT[:, 3, :]
            hold = {}
            wgrp = None
            for blk in range(27):
                if blk % 4 == 0:
                    nb_ = min(4, 27 - blk)
                    wgrp = load_w(sA3[:, :, blk * 128:(blk + nb_) * 128], 8, nb_ * 128)
                pf, pb = psum.get()
                for kb in range(8):
                    mm(pf, wgrp[:, kb, (blk % 4) * 128:(blk % 4 + 1) * 128], hT[:, kb, :], kb == 0, kb == 7)
                if blk < 16:
                    if blk % 2 == 0:
                        hold["main"] = pf
                        continue
                    main, part = hold["main"], pf
                    if blk < 6:
                        dst, cs, sn, scl = qT[:, blk // 2, :], cos64, sin64, 0.125
                    elif blk < 8:
                        dst, cs, sn, scl = kT[:, tok0:tok0 + TC], cos64, sin64, 1.0
                    elif blk < 14:
                        dst, cs, sn, scl = qiT[:, (blk - 8) // 2, :], cos32, sin32, 1.0
                    else:
                        dst, cs, sn, scl = kiT[:, tok0:tok0 + TC], cos32, sin32, 1.0
                    t1 = tmp.get()[:, 0:TC]
                    t2 = tmp.get()[:, 0:TC]
                    stt(t1, main, scl, cs, ALU.mult, ALU.mult)
                    stt(t2, part, scl, sn, ALU.mult, ALU.mult)
                    tt("pool", dst, t1, t2, ALU.add)
                elif blk < 25:
                    i3, kind = (blk - 16) // 3, (blk - 16) % 3
                    if kind == 0:
                        cgs = tmp.get()[:, 0:TC]
                        cp("act", cgs, pf)
                        hold["cgs"] = cgs
                    elif kind == 1:
                        cu = tmp.get()
                        cp("pool", cu[:, 0:2], chalo[:, i3, :])
                        tt("dve", cu[:, 2:2 + TC], pf, hold["cgs"], ALU.mult)
                        cp("pool", chalo[:, i3, :], cu[:, TC:TC + 2])
                        acc = tmp.get()[:, 0:TC]
                        ts("pool", acc, cu[:, 2:2 + TC], cws[:, i3 * 3 + 2:i3 * 3 + 3], 0.0, ALU.mult, ALU.add)
                        stt(acc, cu[:, 1:1 + TC], cws[:, i3 * 3 + 1:i3 * 3 + 2], acc, ALU.mult, ALU.add)
                        stt(acc, cu[:, 0:TC], cws[:, i3 * 3:i3 * 3 + 1], acc, ALU.mult, ALU.add)
                        hold["acc"] = acc
                    else:
                        tt("dve", convT[:, i3, :], pf, hold["acc"], ALU.mult)
                else:
                    act(qcT[:, blk - 25, :], pf, AF.Copy, scale=0.125)
            for tix in range(4):
                pf, pb = psum.get()
                for kb in range(8):
                    mm(pf[:, 0:136], hT[:, kb, tix * 128:(tix + 1) * 128], wV[:, kb, :], kb == 0, kb == 7)
                cp("act", vtok[:, j * 4 + tix, :, 0:64], pf[:, 0:128].re("p (g d) -> p g d", g=2))
                cp("dve", wi_sb[:, tix, :], pf[:, 128:136])
            if seq == 0 and j == 0:
                dump("qT", qT, [128, 3, TC], BF16)
                dump("kT", kT[:, 0:TC], [128, TC], BF16)
                dump("qiT", qiT, [128, 3, TC], BF16)
                dump("kiT", kiT[:, 0:TC], [128, TC], BF16)
                dump("convT", convT, [128, 3, TC], BF16)

            for i in range(4):
                b = 4 * j + i
                Nb = 128 * (b + 1)
                sc = scores[i]
                for c0 in range(0, Nb, 512):
                    w = min(512, Nb - c0)
                    for h in range(8):
                        pf, pb = psum.get()
                        r0 = 32 * (h % 3)
                        mm(pf[:, 0:w], qiT[r0:r0 + 32, h // 3, i * 128:(i + 1) * 128], kiT[r0:r0 + 32, c0:c0 + w],
                           True, True)
                        rb = tmp.get()[:, 0:w]
                        act(rb, pf[:, 0:w], AF.Relu)
                        if h == 0:
                            ts("dve", sc[:, c0:c0 + w], rb, wi_sb[:, i, 0:1], None, ALU.mult)
                        else:
                            stt(sc[:, c0:c0 + w], rb, wi_sb[:, i, h:h + 1], sc[:, c0:c0 + w], ALU.mult, ALU.add)
                da = sc[:, Nb - 128:Nb].ap
                dv = sc[:, Nb - 128:Nb]
                C.op("pool", lambda e, da=da: e.affine_select(out=da, in_=da, pattern=[[-1, 128]],
                                                               compare_op=ALU.is_ge, fill=fillreg(e, -BIG), base=0,
                                                               channel_multiplier=1), [dv], [dv])
            if seq == 0 and j == NCH - 1:
                dump("sc0", scores[0], [128, 2048])
            chains = []
            mode = {}
            for i in range(4):
                b = 4 * j + i
                Nb = 128 * (b + 1)
                cs_ = ch_s[i]
                sc = scores[i]
                if Nb <= TOPK:
                    memset("pool", cs_["nm"], 0.5 * BIG)
                    mode[i] = "act"
                    continue
                mode[i] = "act" if len(chains) % 2 == 0 else "dve"
                chains.append(i)
                reduce(cs_["mx"], sc[:, 0:Nb], ALU.max)
                reduce(cs_["mn"], sc[:, 0:Nb - 128], ALU.min)
                if mode[i] == "act":
                    tt("dve", cs_["nr"], cs_["mn"], cs_["mx"], ALU.subtract)
                    ts("dve", cs_["nst"], pw2, cs_["nr"], None, ALU.mult)
                    ts("dve", cs_["t"], cs_["mn"], -0.5, None, ALU.mult)
                    stt(cs_["nm"], cs_["mx"], -0.5, cs_["t"], ALU.mult, ALU.add)
                else:
                    tt("dve", cs_["nr"], cs_["mx"], cs_["mn"], ALU.subtract)
                    ts("dve", cs_["nst"], pw2, cs_["nr"], 2.0, ALU.mult, ALU.mult)
                    ts("dve", cs_["t"], cs_["mn"], 0.5, None, ALU.mult)
                    stt(cs_["nm"], cs_["mx"], 0.5, cs_["t"], ALU.mult, ALU.add)
            MTf = MT.re("p a b -> p (a b)")
            for k in range(KIT):
                for i in chains:
                    b = 4 * j + i
                    Nb = 128 * (b + 1)
                    cs_ = ch_s[i]
                    if mode[i] == "act":
                        act(Mb[:, 0:Nb], scores[i][:, 0:Nb], AF.Sign, bias=cs_["nm"], accum=cs_["S"])
                        act(cs_["pm"], cs_["S"], AF.Sign, bias=float(Nb - 2 * TOPK) + 0.5)
                    else:
                        ts("dve", MTf[:, 0:Nb], scores[i][:, 0:Nb], cs_["nm"], None, ALU.is_ge, ALU.add,
                           accum=cs_["S"])
                        ts("dve", cs_["pm"], cs_["S"], float(TOPK) - 0.5, -0.5, ALU.is_ge, ALU.add)
                    if mode[i] == "act":
                        ts("pool", cs_["nm"], cs_["pm"], cs_["nst"][:, k:k + 1], cs_["nm"], ALU.mult, ALU.add)
                    else:
                        stt(cs_["nm"], cs_["pm"], cs_["nst"][:, k:k + 1], cs_["nm"], ALU.mult, ALU.add)
            if seq == 0 and j == NCH - 1:
                dump("nm0", ch_s[0]["nm"], [128, 1])
            for i in range(4):
                b = 4 * j + i
                Nb = 128 * (b + 1)
                cs_ = ch_s[i]
                if mode[i] == "act":
                    ts("dve", Mb[:, 0:Nb], scores[i][:, 0:Nb], cs_["nm"], 0.0, ALU.add, ALU.is_ge)
                else:
                    ts("dve", Mb[:, 0:Nb], scores[i][:, 0:Nb], cs_["nm"], None, ALU.is_ge)
                for s0 in range(0, b + 1, 8):
                    n = min(8, b + 1 - s0)
                    pf, pb = psum.get()
                    for k in range(n):
                        tr(pb[:, k * 128:(k + 1) * 128], Mb[:, (s0 + k) * 128:(s0 + k + 1) * 128])
                    cp("act", MT[:, s0:s0 + n, :], pb[:, 0:n * 128].re("p (s t) -> p s t", s=n))
                po, _ = psB.get()
                units = [(h, s0) for h in range(6) for s0 in range(0, b + 1, 4)]

                def qk(u):
                    h, s0 = u
                    g, bk = h // 3, h % 3
                    n = min(4, b + 1 - s0)
                    pl, _ = psum.get()
                    for k in range(n):
                        st = s0 + k
                        mm(pl[:, k * 128:(k + 1) * 128], kT[g * 64:(g + 1) * 64, st * 128:(st + 1) * 128],
                           qT[g * 64:(g + 1) * 64, bk, i * 128:(i + 1) * 128], True, True)
                    return pl

                LA = 2
                pend = [qk(units[k_]) for k_ in range(min(LA, len(units)))]
                for ui, u in enumerate(units):
                    pl = pend.pop(0)
                    if ui + LA < len(units):
                        pend.append(qk(units[ui + LA]))
                    h, s0 = u
                    g = h // 3
                    n = min(4, b + 1 - s0)
                    et = ebuf.get()
                    act(et[:, 0:n * 128], pl[:, 0:n * 128], AF.Exp)
                    etm = ebuf.get()
                    tt("pool" if ui % 2 == 0 else "dve", etm[:, 0:n * 128].re("p (s t) -> p s t", s=n),
                       et[:, 0:n * 128].re("p (s t) -> p s t", s=n), MT[:, s0:s0 + n, :], ALU.mult)
                    for k in range(n):
                        st = s0 + k
                        mm(po[:, h * 65:(h + 1) * 65], etm[:, k * 128:(k + 1) * 128], vtok[:, st, g, :],
                           st == 0, st == b)
                rden = rden_a.get()
                recip(rden[:, 0:6], po[:, 0:390].re("p (h c) -> p h c", c=65)[:, :, 64])
                for h in range(6):
                    col = (h % 3) * 128 + (h // 3) * 64
                    act(att_tok[:, col:col + 64], po[:, h * 65:h * 65 + 64], AF.Copy, scale=rden[:, h:h + 1])
                pf, pb = psum.get()
                for bk in range(3):
                    tr(pb[:, bk * 128:(bk + 1) * 128], att_tok[:, bk * 128:(bk + 1) * 128])
                cp("dve", attT[:, :, i * 128:(i + 1) * 128], pb[:, 0:384].re("p (k t) -> p k t", k=3))
            if seq == 0 and j == NCH - 1:
                dump("attT", attT, [128, 3, TC], BF16)

            pms = [psB.get()[0] for _ in range(4)]
            for h in range(4):
                bk, p0 = h // 2, (h % 2) * 64
                ems = []
                for mt in range(2):
                    pl, _ = psum.get()
                    mm(pl, kmT[p0:p0 + 64, bk, mt * 128:(mt + 1) * 128], qcT[p0:p0 + 64, bk, :], True, True)
                    em = ebuf.get()
                    act(em, pl, AF.Exp)
                    ems.append(em)
                for tix in range(4):
                    for mt in range(2):
                        mm(pms[tix][:, h * 65:(h + 1) * 65], ems[mt][:, tix * 128:(tix + 1) * 128],
                           vmtok[:, mt, h, :], mt == 0, mt == 1)
            for tix in range(4):
                rden = rden_a.get()
                recip(rden[:, 0:4], pms[tix][:, 0:260].re("p (h c) -> p h c", c=65)[:, :, 64])
                for h in range(4):
                    act(mem_tok[:, h * 64:(h + 1) * 64], pms[tix][:, h * 65:h * 65 + 64], AF.Copy,
                        scale=rden[:, h:h + 1])
                pf, pb = psum.get()
                for bk in range(2):
                    tr(pb[:, bk * 128:(bk + 1) * 128], mem_tok[:, bk * 128:(bk + 1) * 128])
                cp("dve", memattT[:, :, tix * 128:(tix + 1) * 128], pb[:, 0:256].re("p (k t) -> p k t", k=2))
            if seq == 0 and j == 0:
                dump("memattT", memattT, [128, 2, TC], BF16)

            load(x1, x_d[seq, tok0:tok0 + TC, :].re("(t p) n -> p t n", p=128))
            for nb in range(8):
                wg = load_w(sG3[:, :, nb * 384:(nb + 1) * 384], 8, 384)
                pY = [psB.get()[0] for _ in range(3)]
                for kb in range(3):
                    mm(pY[0], wAO[:, kb, nb * 128:(nb + 1) * 128], attT[:, kb, :], kb == 0, kb == 2)
                for kb in range(3):
                    mm(pY[1], wCO[:, kb, nb * 128:(nb + 1) * 128], convT[:, kb, :], kb == 0, kb == 2)
                for kb in range(2):
                    mm(pY[2], wMO[:, kb, nb * 128:(nb + 1) * 128], memattT[:, kb, :], kb == 0, kb == 1)
                G = []
                for br in range(3):
                    pg, _ = psum.get()
                    for kb in range(8):
                        mm(pg, wg[:, kb, br * 128:(br + 1) * 128], hT[:, kb, :], kb == 0, kb == 7)
                    g_ = tmp.get()[:, 0:TC]
                    act(g_, pg, AF.Sigmoid, bias=bgate[:, nb * 3 + br:nb * 3 + br + 1])
                    G.append(g_)
                m = tmp.get()[:, 0:TC]
                t = tmp.get()[:, 0:TC]
                tt("dve", m, pY[0], G[0], ALU.mult)
                tt("dve", t, pY[1], G[1], ALU.mult)
                tt("pool", m, m, t, ALU.add)
                tt("dve", t, pY[2], G[2], ALU.mult)
                tt("pool", mergedT[:, nb, :], m, t, ALU.add)
            for n2 in range(2):
                wo = load_w(sO3[:, :, n2 * 512:(n2 + 1) * 512], 8, 512)
                for tix in range(4):
                    pf, _ = psum.get()
                    for kb in range(8):
                        mm(pf, mergedT[:, kb, tix * 128:(tix + 1) * 128], wo[:, kb, :], kb == 0, kb == 7)
                    xs = x1[:, tix, n2 * 512:(n2 + 1) * 512]
                    tt("dve", xs, pf, xs, ALU.add)
            if seq == 0 and j == 0:
                dump("x1", x1, [128, 4, D])

            if j + 1 < NCH:
                stage1(j + 1)
            rms_to_T_multi([x1[:, tix, :] for tix in range(4)], gffn_c, h2T, [0, 1, 2, 3])
            ffn_tail = [None]
            for grp in range(11):
                wu = load_w(sU3[:, :, grp * 512:(grp + 1) * 512], 8, 512)
                for pp in range(2):
                    p = grp * 2 + pp
                    pfs, ubs, accs, ubms = [], [], [], []
                    for which in range(2):
                        pf, _ = psum8.get()
                        for kb in range(8):
                            mm(pf, wu[:, kb, (pp * 2 + which) * 128:(pp * 2 + which + 1) * 128], h2T[:, kb, :],
                               kb == 0, kb == 7)
                        pfs.append(pf)
                    for which in range(2):
                        q = p * 2 + which
                        ub, ubh, ubm = ffn_ub.get()
                        cp("pool", ubh, fhalo[:, q, :])
                        ubs.append(ub)
                        ubms.append(ubm)
                    for which in range(2):
                        q = p * 2 + which
                        cp("act", ubms[which], pfs[which])
                        acc = ffn_acc.get()
                        act(acc, pfs[which], AF.Copy, scale=cwf[:, q * 3 + 2:q * 3 + 3])
                        accs.append(acc)
                    for which in range(2):
                        q = p * 2 + which
                        cp("pool", fhalo[:, q, :], ubms[which][:, TC - 2:TC])
                    for tap in (1, 0):
                        for which in range(2):
                            q = p * 2 + which
                            stt(accs[which], ubs[which][:, tap:tap + TC], cwf[:, q * 3 + tap:q * 3 + tap + 1],
                                accs[which], ALU.mult, ALU.add)
                    if ffn_tail[0] is not None:
                        ffn_tail[0]()

                    def tail(p=p, accs=accs):
                        sg = ffn_acc.get()
                        act(sg, accs[0], AF.Silu)
                        tt("pool", actT[:, p, :], sg, accs[1], ALU.mult)
                    ffn_tail[0] = tail
            ffn_tail[0]()
            ffn_tail[0] = None
            for n2 in range(2):
                pacc = [psB.get()[0] for _ in range(4)]
                for kg in range(0, 22, 8):
                    nk = min(8, 22 - kg)
                    wd = load_w(sD3[:, kg:kg + nk, n2 * 512:(n2 + 1) * 512], nk, 512)
                    for kk in range(nk):
                        kb = kg + kk
                        for tix in range(4):
                            mm(pacc[tix], actT[:, kb, tix * 128:(tix + 1) * 128], wd[:, kk, :], kb == 0, kb == 21)
                for tix in range(4):
                    xs = x1[:, tix, n2 * 512:(n2 + 1) * 512]
                    tt("dve", xs, pacc[tix], xs, ALU.add)
            rstd = rms_stats([x1[:, tix, :] for tix in range(4)])
            for tix in range(4):
                ob = obuf.get()
                stt(ob, x1[:, tix, :], rstd[:, tix:tix + 1], gfin, ALU.mult, ALU.mult)
                dst = out_d[seq, tok0 + tix * 128:tok0 + (tix + 1) * 128, :]
                C.dma("sp", dst, ob, ob.bufs[0], reads=[ob], writes=[dst])

    finals = [o.bufs[0] for o in obuf.items] + [d.bufs[0] for d in dbg_out.values()] + [out_d.bufs[0]]
    C.emit(finals)
    return nc, dbg_out


def _layout_weights(S, g_mix, w_in, b_gate, conv_w_short, w_att_out, w_conv_out, w_mem_out, w_o, g_mem,
                    w_mem_kv, g_ffn, w_up, conv_w_ffn, w_down, g_final):
    f = np.float32
    W = np.asarray(w_in[0], f)
    oq, ok, ov, oqi, oki, owi, oc, oqc, og = 0, 384, 512, 640, 896, 928, 936, 2088, 2344

    def partner(base, nheads, hd, half):
        idx = []
        for h in range(nheads):
            for d in range(hd):
                if d < half:
                    pd = d + half
                elif d < 2 * half:
                    pd = d - half
                else:
                    pd = d
                idx.append(base + h * hd + pd)
        return np.array(idx)

    cols = []
    qmain = [np.concatenate([np.arange(oq + i * 64, oq + (i + 1) * 64), np.arange(oq + (3 + i) * 64, oq + (4 + i) * 64)])
             for i in range(3)]
    qpart_all = partner(oq, 6, 64, 8)
    for i in range(3):
        cols.append(qmain[i])
        cols.append(qpart_all[qmain[i] - oq])
    cols.append(np.arange(ok, ok + 128))
    cols.append(partner(ok, 2, 64, 8))
    qip = partner(oqi, 8, 32, 4)
    for i in range(3):
        idx = np.arange(i * 96, i * 96 + 128) % 256
        cols.append(oqi + idx)
        cols.append(qip[idx])
    cols.append(np.tile(np.arange(oki, oki + 32), 4))
    cols.append(np.tile(partner(oki, 1, 32, 4), 4))
    for i in range(3):
        cols.append(np.arange(oc + 384 + i * 128, oc + 384 + (i + 1) * 128))
        cols.append(np.arange(oc + 768 + i * 128, oc + 768 + (i + 1) * 128))
        cols.append(np.arange(oc + i * 128, oc + (i + 1) * 128))
    cols.append(np.arange(oqc, oqc + 128))
    cols.append(np.arange(oqc + 128, oqc + 256))
    wA = np.ascontiguousarray(W[:, np.concatenate(cols)])
    wV = np.ascontiguousarray(W[:, np.concatenate([np.arange(ov, ov + 128), np.arange(owi, owi + 8)])])
    gcols = np.concatenate([np.arange(og + br * 1024 + nb * 128, og + br * 1024 + (nb + 1) * 128)
                            for nb in range(8) for br in range(3)])
    wG = np.ascontiguousarray(W[:, gcols])
    bg = np.asarray(b_gate[0], f)[gcols - og].reshape(24, 128).T
    arow = np.concatenate([np.concatenate([np.arange(i * 64, (i + 1) * 64), np.arange((3 + i) * 64, (4 + i) * 64)])
                           for i in range(3)])
    wAO = np.ascontiguousarray(np.asarray(w_att_out[0], f)[arow])
    ucols = np.concatenate([np.concatenate([np.arange(p * 128, (p + 1) * 128), np.arange(DFF + p * 128, DFF + (p + 1) * 128)])
                            for p in range(22)])
    wU = np.ascontiguousarray(np.asarray(w_up[0], f)[:, ucols])
    cwf = np.asarray(conv_w_ffn[0], f)[:, ucols]
    cwf = cwf.reshape(3, 44, 128).transpose(2, 1, 0).reshape(128, 132)
    cws = np.asarray(conv_w_short[0], f).reshape(3, 3, 128).transpose(2, 1, 0).reshape(128, 9)
    gcol = np.concatenate([np.asarray(g, f).reshape(8, 128).T for g in (g_mix[0], g_mem[0], g_ffn[0])], axis=1)
    gfin = np.broadcast_to(np.asarray(g_final, f)[None, :], (128, D))
    pos = np.arange(S, dtype=f)

    def tables(hd, half):
        inv = (ROPE_THETA ** (-(np.arange(half, dtype=f) / f(half)))).astype(f)
        ang = (pos[:, None] * inv[None, :]).astype(f)
        c, s = np.cos(ang).astype(f), np.sin(ang).astype(f)
        ct = np.ones((128, S), f)
        st = np.zeros((128, S), f)
        for p in range(128):
            d = p % hd
            if d < half:
                ct[p] = c[:, d]
                st[p] = -s[:, d]
            elif d < 2 * half:
                ct[p] = c[:, d - half]
                st[p] = s[:, d - half]
        return ct, st

    c64, s64 = tables(64, 8)
    c32, s32 = tables(32, 4)
    rope = np.stack([c64, s64, c32, s32])
    pw2 = np.broadcast_to((2.0 ** -(np.arange(KIT, dtype=f) + 2))[None, :], (128, KIT))
    c = np.ascontiguousarray
    return dict(wA=wA, wV=wV, wG=wG, wKV=c(np.asarray(w_mem_kv[0], f)), wAO=wAO,
                wCO=c(np.asarray(w_conv_out[0], f)), wMO=c(np.asarray(w_mem_out[0], f)),
                wO=c(np.asarray(w_o[0], f)), wU=wU, wD=c(np.asarray(w_down[0], f)),
                gcol=c(gcol), gfin=c(gfin), bgate=c(bg), cws=c(cws), cwf=c(cwf), rope=c(rope), pw2=c(pw2))


_NC_CACHE = {}


def run(x, mem, dbg=None, **wts):
    x = np.asarray(x, np.float32)
    mem = np.asarray(mem, np.float32)
    B, S, _ = x.shape
    NSEQ = B // NCORES
    key = (S, NSEQ, tuple(sorted(dbg)) if dbg else None)
    if key not in _NC_CACHE:
        _NC_CACHE[key] = build(S, NSEQ, dbg)
    nc, dbg_out = _NC_CACHE[key]
    shared = _layout_weights(S, **wts)
    in_maps = []
    for c in range(NCORES):
        m = dict(shared)
        m["x"] = np.ascontiguousarray(x[c * NSEQ:(c + 1) * NSEQ])
        m["mem"] = np.ascontiguousarray(mem[c * NSEQ:(c + 1) * NSEQ])
        in_maps.append(m)
    res = run_bass_kernel_spmd(nc, in_maps, core_ids=list(range(NCORES)))
    out = np.concatenate([np.asarray(r["out"]) for r in res.results], axis=0).astype(np.float32)
    if dbg:
        return out, {k: np.asarray(res.results[0]["dbg_" + k]) for k in dbg_out}
    return out


def kernel(x, mem, g_mix, w_in, b_gate, conv_w_short, w_att_out, w_conv_out, w_mem_out, w_o, g_mem,
           w_mem_kv, g_ffn, w_up, conv_w_ffn, w_down, g_final):
    return run(x, mem, g_mix=g_mix, w_in=w_in, b_gate=b_gate, conv_w_short=conv_w_short, w_att_out=w_att_out,
               w_conv_out=w_conv_out, w_mem_out=w_mem_out, w_o=w_o, g_mem=g_mem, w_mem_kv=w_mem_kv, g_ffn=g_ffn,
               w_up=w_up, conv_w_ffn=conv_w_ffn, w_down=w_down, g_final=g_final)
```

```python
import numpy as np
import concourse.bass as bass
import concourse.mybir as mybir
from concourse.bass_utils import run_bass_kernel_spmd

F32 = mybir.dt.float32
BF16 = mybir.dt.bfloat16
AF = mybir.ActivationFunctionType
ALU = mybir.AluOpType
AX = mybir.AxisListType

D = 1024
NCORES = 8
TC = 512
MEM = 256
DFF = 2816
EPS = 1e-6
BIG = 1.0e30
KIT = 12
SAME_ENGINE_SKIP = ("pe",)
ROPE_THETA = 500000.0


class Buf:
    __slots__ = ("w", "r", "sem", "cnt", "name")

    def __init__(self, name=""):
        self.w = {}
        self.r = {}
        self.sem = None
        self.cnt = 0
        self.name = name


class V:
    def __init__(self, bufs, ap):
        self.bufs = bufs
        self.ap = ap

    def __getitem__(self, k):
        return V(self.bufs, self.ap[k])

    def re(self, pat, **kw):
        return V(self.bufs, self.ap.rearrange(pat, **kw))

    def bcast(self, shape):
        return V(self.bufs, self.ap.broadcast_to(shape))


class Eng:
    def __init__(self, name, sem):
        self.name = name
        self.sem = sem
        self.ops = []


class Ctx:
    def __init__(self, nc):
        self.nc = nc
        self.E = {}
        for n in ("pe", "act", "dve", "pool", "sp"):
            self.E[n] = Eng(n, nc.alloc_semaphore("sem_" + n))
        self.nsem = 5

    def _need(self, en, reads, writes, is_dma):
        need = {}

        def add(d, skip_own):
            for k, val in d.items():
                if skip_own and k == ("e", en):
                    continue
                if need.get(k, -1) < val:
                    need[k] = val

        for v in reads:
            for b in v.bufs:
                add(b.w, False)
        skip = (not is_dma) and en in SAME_ENGINE_SKIP
        for v in writes:
            for b in v.bufs:
                add(b.w, skip)
                add(b.r, skip)
        return need

    def op(self, en, fn, reads=(), writes=()):
        E = self.E[en]
        need = self._need(en, reads, writes, False)
        for k, val in need.items():
            if k[0] == "e":
                self.E[k[1]].ops[val][3] = True
        idx = len(E.ops)
        E.ops.append([need, fn, None, False])
        key = ("e", en)
        for v in reads:
            for b in v.bufs:
                b.r[key] = idx
        for v in writes:
            for b in v.bufs:
                b.w = {key: idx}
                b.r = {}

    def dma(self, q, out, in_, owner, reads=(), writes=(), merge_w=False):
        E = self.E[q]
        need = self._need(q, reads, () if merge_w else writes, True)
        for k, val in need.items():
            if k[0] == "e":
                self.E[k[1]].ops[val][3] = True
        if owner.sem is None:
            owner.sem = self.nc.alloc_semaphore("dsem%d" % self.nsem)
            self.nsem += 1
        owner.cnt += 16
        key = ("d", id(owner), owner)
        oap, iap = out.ap, in_.ap
        E.ops.append([need, lambda e: e.dma_start(out=oap, in_=iap), (owner.sem, 16), True])
        for v in reads:
            for b in v.bufs:
                b.r[key] = owner.cnt
        for v in writes:
            for b in v.bufs:
                if merge_w:
                    b.w[key] = owner.cnt
                else:
                    b.w = {key: owner.cnt}
                    b.r = {}

    def emit(self, final_waits):
        nc = self.nc
        num = {}
        for n, E in self.E.items():
            c = 0
            m = []
            for o in E.ops:
                if o[2] is None and o[3]:
                    c += 1
                m.append(c)
            num[n] = m
        plan = {"pe": "tensor", "act": "scalar", "dve": "vector", "pool": "gpsimd", "sp": "sync"}
        with nc.Block() as block:
            for n, attr in plan.items():
                E = self.E[n]

                def body(e, E=E, n=n):
                    waited = {}
                    def do_wait(k, val):
                        if k[0] == "e":
                            sem = self.E[k[1]].sem
                            v = num[k[1]][val]
                        else:
                            sem = k[2].sem
                            v = val
                        kk = id(sem)
                        if waited.get(kk, 0) >= v:
                            return
                        waited[kk] = v
                        e.wait_ge(sem, v)
                    for need, fn, inc, need_inc in E.ops:
                        for k, val in need.items():
                            do_wait(k, val)
                        ins = fn(e)
                        if inc is not None:
                            ins.then_inc(inc[0], inc[1])
                        elif need_inc:
                            ins.then_inc(E.sem, 1)
                    if n == "sp":
                        for b in final_waits:
                            for k, val in list(b.r.items()) + list(b.w.items()):
                                do_wait(k, val)

                getattr(block, attr)(body)


class RPool:
    def __init__(self, items):
        self.items = items
        self.i = 0

    def get(self):
        it = self.items[self.i % len(self.items)]
        self.i += 1
        return it


def build(S, NSEQ, dbg=None):
    assert S % TC == 0
    NCH = S // TC
    NT = S // 128
    TOPK = min(256, S // 4)
    nc = bass.Bass("TRN2", target_bir_lowering=False)
    C = Ctx(nc)

    def dram(name, shape, dt=F32, kind="ExternalInput"):
        return V([Buf(name)], nc.dram_tensor(name, shape, dt, kind=kind).ap())

    x_d = dram("x", [NSEQ, S, D])
    mem_d = dram("mem", [NSEQ, MEM, D])
    out_d = dram("out", [NSEQ, S, D], kind="ExternalOutput")
    wspec = [("wA", 1024, 3456), ("wV", 1024, 136), ("wG", 1024, 3072), ("wKV", 1024, 512),
             ("wAO", 384, 1024), ("wCO", 384, 1024), ("wMO", 256, 1024), ("wO", 1024, 1024),
             ("wU", 1024, 5632), ("wD", 2816, 1024)]
    wf, ws = {}, {}
    for n, k, m in wspec:
        wf[n] = dram(n, [k, m])
        ws[n] = dram("s" + n, [k, m], BF16, kind="Internal")
    gcol_d = dram("gcol", [128, 24])
    gfin_d = dram("gfin", [128, D])
    bgate_d = dram("bgate", [128, 24])
    cws_d = dram("cws", [128, 9])
    cwf_d = dram("cwf", [128, 132])
    rope_d = dram("rope", [4, 128, S])
    pw2_d = dram("pw2", [128, KIT])

    def sb(name, shape, dt, nbuf=None):
        t = nc.alloc_sbuf_tensor(name, shape, dt)
        return V([Buf(name)], t[:]), t

    ident, _ = sb("ident", [128, 128], BF16)
    gcol, _ = sb("gcol_s", [128, 24], F32)
    gfin, _ = sb("gfin_s", [128, D], F32)
    bgate, _ = sb("bgate_s", [128, 24], F32)
    cws, _ = sb("cws_s", [128, 9], F32)
    cwf, _ = sb("cwf_s", [128, 132], F32)
    pw2, _ = sb("pw2_s", [128, KIT], F32)
    ropeT, _ = sb("ropeT", [128, 4, TC], F32)
    wV, _ = sb("wV_s", [128, 8, 136], BF16)
    wAO, _ = sb("wAO_s", [128, 3, D], BF16)
    wCO, _ = sb("wCO_s", [128, 3, D], BF16)
    wMO, _ = sb("wMO_s", [128, 2, D], BF16)
    xpool = RPool([sb("xt%d" % i, [128, D], F32)[0] for i in range(2)])
    x1, _ = sb("x1", [128, 4, D], F32)
    hbs = RPool([sb("hb%d" % i, [128, D], BF16)[0] for i in range(2)])
    hb = hbs.items[0]
    hT, _ = sb("hT", [128, 8, TC], BF16)
    h2T, _ = sb("h2T", [128, 8, TC], BF16)
    _, qm_t = sb("qm", [128, 4096], BF16)
    bq, bqi, bqc, bqx = Buf("qT"), Buf("qiT"), Buf("qcT"), Buf("qx")
    qT = V([bq], qm_t[:, 0:1536]).re("p (k t) -> p k t", k=3)
    qiT = V([bqi], qm_t[:, 1536:3072]).re("p (k t) -> p k t", k=3)
    qcT = V([bqc], qm_t[:, 3072:4096]).re("p (k t) -> p k t", k=2)
    mergedT = V([bq, bqi, bqc, bqx], qm_t[:, 0:4096]).re("p (k t) -> p k t", k=8)
    convT, _ = sb("convT", [128, 3, TC], BF16)
    attT, _ = sb("attT", [128, 3, TC], BF16)
    memattT, _ = sb("memattT", [128, 2, TC], BF16)
    kT, _ = sb("kT", [128, S], BF16)
    kiT, _ = sb("kiT", [128, S], BF16)
    vtok, _ = sb("vtok", [128, NT, 2, 65], BF16)
    kmT, _ = sb("kmT", [128, 2, MEM], BF16)
    vmtok, _ = sb("vmtok", [128, 2, 4, 65], BF16)
    wi_sb, _ = sb("wi_sb", [128, 4, 8], F32)
    sc_t = nc.alloc_sbuf_tensor("scact", [128, 16384], BF16)
    sc_f = sc_t.bitcast(F32)
    scb = [Buf("sc%d" % i) for i in range(4)]
    scores = [V([scb[i]], sc_f[:, i * 2048:(i + 1) * 2048]) for i in range(4)]
    actT = V(scb, sc_t[:, 0:22 * TC]).re("p (k t) -> p k t", k=22)
    Mb, Mb_t = sb("Mb", [128, 2048], BF16)
    MT, MT_t = sb("MT", [128, 16, 128], BF16)
    ub_items = []
    for i in range(4):
        t_ = nc.alloc_sbuf_tensor("ub%d" % i, [128, 516], F32)
        bh_, bm_ = Buf("ubh%d" % i), Buf("ubm%d" % i)
        ub_items.append((V([bh_, bm_], t_[:]), V([bh_], t_[:, 0:2]), V([bm_], t_[:, 2:2 + TC])))
    tmp = RPool([sb("tmp%d" % i, [128, 516], F32)[0] for i in range(8)])
    Mb_f = Mb_t.bitcast(F32)
    MT_f = MT_t.bitcast(F32)[:].rearrange("p a b -> p (a b)")
    alias_acc = [V(Mb.bufs, Mb_f[:, 0:512]), V(Mb.bufs, Mb_f[:, 512:1024]),
                 V(MT.bufs, MT_f[:, 0:512]), V(MT.bufs, MT_f[:, 512:1024])]
    ffn_ub = RPool(ub_items)
    ffn_acc = RPool([t_[:, 0:512] for t_ in tmp.items] + alias_acc)
    ebuf = RPool([sb("eb%d" % i, [128, 512], BF16)[0] for i in range(6)])
    att_tok, _ = sb("att_tok", [128, 384], BF16)
    mem_tok, _ = sb("mem_tok", [128, 256], BF16)
    chalo, _ = sb("chalo", [128, 3, 2], F32)
    fhalo, _ = sb("fhalo", [128, 44, 2], F32)
    obuf = RPool([sb("ob%d" % i, [128, D], F32)[0] for i in range(1)])
    wbig = RPool([sb("wb%d" % i, [128, 4096], BF16)[0] for i in range(3)])
    _, sm_t = sb("smalls", [128, 160], F32)
    smi = [0]

    def small(n=1):
        i = smi[0]
        smi[0] += n
        assert smi[0] <= 160
        return V([Buf("sm%d" % i)], sm_t[:, i:i + n])

    rms_s = RPool([[small(4), small(4), small(4), small(4)] for _ in range(3)])
    ch_s = [dict(mx=small(), mn=small(), nr=small(), nst=small(KIT), nm=small(), S=small(), pm=small(), t=small())
            for _ in range(4)]
    rden_a = RPool([small(8) for _ in range(2)])

    psl = []
    for i in range(8):
        t = nc.alloc_psum_tensor("ps%d" % i, [128, 512], F32)
        b = Buf("ps%d" % i)
        psl.append((V([b], t[:]), V([b], t.bitcast(BF16)[:])))
    psum = RPool(psl[0:4])
    psB = RPool(psl[4:8])
    psum8 = RPool(psl)
    print("sbuf bytes remaining:", nc.sbuf_bytes_remaining)

    def mm(out, lhsT, rhs, start, stop):
        o, l, r = out.ap, lhsT.ap, rhs.ap
        C.op("pe", lambda e: e.matmul(o, lhsT=l, rhs=r, start=start, stop=stop), [lhsT, rhs], [out])

    def tr(out, in_):
        o, i, idn = out.ap, in_.ap, ident.ap
        C.op("pe", lambda e: e.transpose(out=o, in_=i, identity=idn), [in_, ident], [out])

    def act(out, in_, func, bias=None, scale=None, accum=None):
        o, i = out.ap, in_.ap
        kw = {}
        rd, wr = [in_], [out]
        if bias is not None:
            if isinstance(bias, V):
                kw["bias"] = bias.ap
                rd.append(bias)
            else:
                kw["bias"] = float(bias)
        if scale is not None:
            if isinstance(scale, V):
                kw["scale"] = scale.ap
                rd.append(scale)
            else:
                kw["scale"] = float(scale)
        if accum is not None:
            kw["accum_out"] = accum.ap
            wr.append(accum)
        C.op("act", lambda e: e.activation(out=o, in_=i, func=func, **kw), rd, wr)

    def tt(en, out, a, b, op):
        o, x, y = out.ap, a.ap, b.ap
        C.op(en, lambda e: e.tensor_tensor(out=o, in0=x, in1=y, op=op), [a, b], [out])

    def ts(en, out, a, s1, s2, op0, op1=None, accum=None):
        o, x = out.ap, a.ap
        rd, wr = [a], [out]
        if isinstance(s1, V):
            rd.append(s1)
            s1 = s1.ap
        if isinstance(s2, V):
            rd.append(s2)
            s2 = s2.ap
        kw = {}
        if op1 is not None:
            kw["op1"] = op1
        if accum is not None:
            kw["accum_out"] = accum.ap
            wr.append(accum)
        C.op(en, lambda e: e.tensor_scalar(out=o, in0=x, scalar1=s1, scalar2=s2, op0=op0, **kw), rd, wr)

    def stt(out, a, s, b, op0, op1):
        o, x, y = out.ap, a.ap, b.ap
        rd = [a, b]
        if isinstance(s, V):
            rd.append(s)
            s = s.ap
        C.op("dve", lambda e: e.scalar_tensor_tensor(out=o, in0=x, scalar=s, in1=y, op0=op0, op1=op1), rd, [out])

    def cp(en, out, in_):
        if en == "act":
            return act(out, in_, AF.Copy)
        o, i = out.ap, in_.ap
        C.op(en, lambda e: e.tensor_copy(out=o, in_=i), [in_], [out])

    def memset(en, out, val):
        o = out.ap
        C.op(en, lambda e: e.memset(o, val), [], [out])

    def recip(out, in_):
        o, i = out.ap, in_.ap
        C.op("dve", lambda e: e.reciprocal(out=o, in_=i), [in_], [out])

    def reduce(out, in_, op):
        o, i = out.ap, in_.ap
        C.op("dve", lambda e: e.tensor_reduce(out=o, in_=i, axis=AX.X, op=op), [in_], [out])

    def load(dst, src, owner=None):
        C.dma("sp", dst, src, owner or dst.bufs[0], reads=[src], writes=[dst])

    def load_w(src3, KB, n):
        slot = wbig.get()
        view = slot[:, 0:KB * n].re("p (k n) -> p k n", k=KB)
        load(view, src3)
        return view

    regcache = {}

    def fillreg(e, val):
        if val not in regcache:
            regcache[val] = e.to_reg(val)
        return regcache[val]

    dbg_out = {}

    def dump(name, v, shape, dt=F32):
        if dbg is None or name not in dbg:
            return
        d = dram("dbg_" + name, shape, dt, kind="ExternalOutput")
        dbg_out[name] = d
        C.dma("sp", d, v, v.bufs[0], reads=[v], writes=[d])

    memset("pool", ident, 0.0)
    ia = ident.ap
    C.op("pool", lambda e: e.affine_select(out=ia, in_=ia, pattern=[[-1, 128]], compare_op=ALU.not_equal,
                                           fill=1.0, base=0, channel_multiplier=1), [ident], [ident])
    for dst, src in ((gcol, gcol_d), (gfin, gfin_d), (bgate, bgate_d), (cws, cws_d), (cwf, cwf_d), (pw2, pw2_d)):
        load(dst, src)
    memset("pool", vtok, 1.0)
    memset("pool", vmtok, 1.0)

    stg_f = RPool(scores)
    stg_b = RPool([Mb, MT.re("p a b -> p (a b)")])
    tiles = []
    for n, k, m in wspec:
        for r0 in range(0, k, 128):
            for c0 in range(0, m, 2048):
                tiles.append((n, r0, c0, min(2048, m - c0)))
    fbufs = {}

    def cv_load(ix):
        n, r0, c0, w = tiles[ix]
        f = stg_f.get()
        fbufs[ix] = f
        load(f[:, 0:w], wf[n][r0:r0 + 128, c0:c0 + w])

    for ix in range(min(3, len(tiles))):
        cv_load(ix)
    for ix, (n, r0, c0, w) in enumerate(tiles):
        f = fbufs.pop(ix)
        b = stg_b.get()
        cp("dve" if ix % 2 == 0 else "pool", b[:, 0:w], f[:, 0:w])
        C.dma("act", ws[n][r0:r0 + 128, c0:c0 + w], b[:, 0:w], b.bufs[0], reads=[b], writes=[ws[n]], merge_w=True)
        if ix + 3 < len(tiles):
            cv_load(ix + 3)
    load(wV, ws["wV"].re("(kb p) n -> p kb n", p=128))
    load(wAO, ws["wAO"].re("(kb p) n -> p kb n", p=128))
    load(wCO, ws["wCO"].re("(kb p) n -> p kb n", p=128))
    load(wMO, ws["wMO"].re("(kb p) n -> p kb n", p=128))
    sA3 = ws["wA"].re("(kb p) n -> p kb n", p=128)
    sG3 = ws["wG"].re("(kb p) n -> p kb n", p=128)
    sKV3 = ws["wKV"].re("(kb p) n -> p kb n", p=128)
    sO3 = ws["wO"].re("(kb p) n -> p kb n", p=128)
    sU3 = ws["wU"].re("(kb p) n -> p kb n", p=128)
    sD3 = ws["wD"].re("(kb p) n -> p kb n", p=128)

    evac_rr = [0]

    def rms_stats(xvs):
        ss, var, sd, rstd = rms_s.get()
        n = len(xvs)
        for i, xv in enumerate(xvs):
            act(hbs.items[i % 2], xv, AF.Square, accum=ss[:, i:i + 1])
        ts("dve", var[:, 0:n], ss[:, 0:n], 1.0 / D, EPS, ALU.mult, ALU.add)
        act(sd[:, 0:n], var[:, 0:n], AF.Sqrt)
        recip(rstd[:, 0:n], sd[:, 0:n])
        return rstd

    def rms_to_T_multi(xvs, gc, dstT, tixs):
        rstd = rms_stats(xvs)
        for i, (xv, tix) in enumerate(zip(xvs, tixs)):
            h_ = hbs.get()
            act(h_, xv, AF.Copy, scale=rstd[:, i:i + 1])
            pf, pb = psum.get()
            for kb in range(8):
                tr(pb[:, kb * 128:(kb + 1) * 128], h_[:, kb * 128:(kb + 1) * 128])
            tt("dve", dstT[:, :, tix * 128:(tix + 1) * 128], pb.re("p (k t) -> p k t", k=8),
               gc.re("p (k o) -> p k o", o=1).bcast([128, 8, 128]), ALU.mult)

    gmix_c, gmem_c, gffn_c = gcol[:, 0:8], gcol[:, 8:16], gcol[:, 16:24]

    for seq in range(NSEQ):
        memhT = h2T
        xts = []
        for mt in range(2):
            xt = xpool.get()
            load(xt, mem_d[seq, mt * 128:(mt + 1) * 128, :])
            xts.append(xt)
        rms_to_T_multi(xts, gmem_c, memhT, [0, 1])
        wkv = load_w(sKV3, 8, 512)
        for blk in range(2):
            pf, pb = psum.get()
            for kb in range(8):
                mm(pf[:, 0:MEM], wkv[:, kb, blk * 128:(blk + 1) * 128], memhT[:, kb, 0:MEM], kb == 0, kb == 7)
            cp("act", kmT[:, blk, :], pf[:, 0:MEM])
        for mt in range(2):
            pf, pb = psum.get()
            for kb in range(8):
                mm(pf[:, 0:256], memhT[:, kb, mt * 128:(mt + 1) * 128], wkv[:, kb, 256:512], kb == 0, kb == 7)
            cp("act", vmtok[:, mt, :, 0:64], pf[:, 0:256].re("p (h d) -> p h d", h=4))
        memset("pool", chalo, 0.0)
        memset("pool", fhalo, 0.0)

        def stage1(j):
            tok0 = j * TC
            load(ropeT, rope_d[:, :, tok0:tok0 + TC].re("f p t -> p f t"))
            for half in range(2):
                xts = []
                for tix in (2 * half, 2 * half + 1):
                    xt = xpool.get()
                    load(xt, x_d[seq, tok0 + tix * 128: tok0 + (tix + 1) * 128, :])
                    xts.append(xt)
                rms_to_T_multi(xts, gmix_c, hT, [2 * half, 2 * half + 1])

        stage1(0)
        for j in range(NCH):
            tok0 = j * TC
            cos64, sin64, cos32, sin32 = ropeT[:, 0, :], ropeT[:, 1, :], ropeT[:, 2, :], ropeT[:, 3, :]
            hold = {}
            wgrp = None
            for blk in range(27):
                if blk % 4 == 0:
                    nb_ = min(4, 27 - blk)
                    wgrp = load_w(sA3[:, :, blk * 128:(blk + nb_) * 128], 8, nb_ * 128)
                pf, pb = psum.get()
                for kb in range(8):
                    mm(pf, wgrp[:, kb, (blk % 4) * 128:(blk % 4 + 1) * 128], hT[:, kb, :], kb == 0, kb == 7)
                if blk < 16:
                    if blk % 2 == 0:
                        hold["main"] = pf
                        continue
                    main, part = hold["main"], pf
                    if blk < 6:
                        dst, cs, sn, scl = qT[:, blk // 2, :], cos64, sin64, 0.125
                    elif blk < 8:
                        dst, cs, sn, scl = kT[:, tok0:tok0 + TC], cos64, sin64, 1.0
                    elif blk < 14:
                        dst, cs, sn, scl = qiT[:, (blk - 8) // 2, :], cos32, sin32, 1.0
                    else:
                        dst, cs, sn, scl = kiT[:, tok0:tok0 + TC], cos32, sin32, 1.0
                    t1 = tmp.get()[:, 0:TC]
                    t2 = tmp.get()[:, 0:TC]
                    stt(t1, main, scl, cs, ALU.mult, ALU.mult)
                    stt(t2, part, scl, sn, ALU.mult, ALU.mult)
                    tt("pool", dst, t1, t2, ALU.add)
                elif blk < 25:
                    i3, kind = (blk - 16) // 3, (blk - 16) % 3
                    if kind == 0:
                        cgs = tmp.get()[:, 0:TC]
                        cp("act", cgs, pf)
                        hold["cgs"] = cgs
                    elif kind == 1:
                        cu = tmp.get()
                        cp("pool", cu[:, 0:2], chalo[:, i3, :])
                        tt("dve", cu[:, 2:2 + TC], pf, hold["cgs"], ALU.mult)
                        cp("pool", chalo[:, i3, :], cu[:, TC:TC + 2])
                        acc = tmp.get()[:, 0:TC]
                        ts("pool", acc, cu[:, 2:2 + TC], cws[:, i3 * 3 + 2:i3 * 3 + 3], 0.0, ALU.mult, ALU.add)
                        stt(acc, cu[:, 1:1 + TC], cws[:, i3 * 3 + 1:i3 * 3 + 2], acc, ALU.mult, ALU.add)
                        stt(acc, cu[:, 0:TC], cws[:, i3 * 3:i3 * 3 + 1], acc, ALU.mult, ALU.add)
                        hold["acc"] = acc
                    else:
                        tt("dve", convT[:, i3, :], pf, hold["acc"], ALU.mult)
                else:
                    act(qcT[:, blk - 25, :], pf, AF.Copy, scale=0.125)
            for tix in range(4):
                pf, pb = psum.get()
                for kb in range(8):
                    mm(pf[:, 0:136], hT[:, kb, tix * 128:(tix + 1) * 128], wV[:, kb, :], kb == 0, kb == 7)
                cp("act", vtok[:, j * 4 + tix, :, 0:64], pf[:, 0:128].re("p (g d) -> p g d", g=2))
                cp("dve", wi_sb[:, tix, :], pf[:, 128:136])
            if seq == 0 and j == 0:
                dump("qT", qT, [128, 3, TC], BF16)
                dump("kT", kT[:, 0:TC], [128, TC], BF16)
                dump("qiT", qiT, [128, 3, TC], BF16)
                dump("kiT", kiT[:, 0:TC], [128, TC], BF16)
                dump("convT", convT, [128, 3, TC], BF16)

            for i in range(4):
                b = 4 * j + i
                Nb = 128 * (b + 1)
                sc = scores[i]
                for c0 in range(0, Nb, 512):
                    w = min(512, Nb - c0)
                    for h in range(8):
                        pf, pb = psum.get()
                        r0 = 32 * (h % 3)
                        mm(pf[:, 0:w], qiT[r0:r0 + 32, h // 3, i * 128:(i + 1) * 128], kiT[r0:r0 + 32, c0:c0 + w],
                           True, True)
                        rb = tmp.get()[:, 0:w]
                        act(rb, pf[:, 0:w], AF.Relu)
                        if h == 0:
                            ts("dve", sc[:, c0:c0 + w], rb, wi_sb[:, i, 0:1], None, ALU.mult)
                        else:
                            stt(sc[:, c0:c0 + w], rb, wi_sb[:, i, h:h + 1], sc[:, c0:c0 + w], ALU.mult, ALU.add)
                da = sc[:, Nb - 128:Nb].ap
                dv = sc[:, Nb - 128:Nb]
                C.op("pool", lambda e, da=da: e.affine_select(out=da, in_=da, pattern=[[-1, 128]],
                                                               compare_op=ALU.is_ge, fill=fillreg(e, -BIG), base=0,
                                                               channel_multiplier=1), [dv], [dv])
            if seq == 0 and j == NCH - 1:
                dump("sc0", scores[0], [128, 2048])
            chains = []
            mode = {}
            for i in range(4):
                b = 4 * j + i
                Nb = 128 * (b + 1)
                cs_ = ch_s[i]
                sc = scores[i]
                if Nb <= TOPK:
                    memset("pool", cs_["nm"], 0.5 * BIG)
                    mode[i] = "act"
                    continue
                mode[i] = "act" if len(chains) % 2 == 0 else "dve"
                chains.append(i)
                reduce(cs_["mx"], sc[:, 0:Nb], ALU.max)
                reduce(cs_["mn"], sc[:, 0:Nb - 128], ALU.min)
                if mode[i] == "act":
                    tt("dve", cs_["nr"], cs_["mn"], cs_["mx"], ALU.subtract)
                    ts("dve", cs_["nst"], pw2, cs_["nr"], None, ALU.mult)
                    ts("dve", cs_["t"], cs_["mn"], -0.5, None, ALU.mult)
                    stt(cs_["nm"], cs_["mx"], -0.5, cs_["t"], ALU.mult, ALU.add)
                else:
                    tt("dve", cs_["nr"], cs_["mx"], cs_["mn"], ALU.subtract)
                    ts("dve", cs_["nst"], pw2, cs_["nr"], 2.0, ALU.mult, ALU.mult)
                    ts("dve", cs_["t"], cs_["mn"], 0.5, None, ALU.mult)
                    stt(cs_["nm"], cs_["mx"], 0.5, cs_["t"], ALU.mult, ALU.add)
            MTf = MT.re("p a b -> p (a b)")
            for k in range(KIT):
                for i in chains:
                    b = 4 * j + i
                    Nb = 128 * (b + 1)
                    cs_ = ch_s[i]
                    if mode[i] == "act":
                        act(Mb[:, 0:Nb], scores[i][:, 0:Nb], AF.Sign, bias=cs_["nm"], accum=cs_["S"])
                        act(cs_["pm"], cs_["S"], AF.Sign, bias=float(Nb - 2 * TOPK) + 0.5)
                    else:
                        ts("dve", MTf[:, 0:Nb], scores[i][:, 0:Nb], cs_["nm"], None, ALU.is_ge, ALU.add,
                           accum=cs_["S"])
                        ts("dve", cs_["pm"], cs_["S"], float(TOPK) - 0.5, -0.5, ALU.is_ge, ALU.add)
                    if mode[i] == "act":
                        ts("pool", cs_["nm"], cs_["pm"], cs_["nst"][:, k:k + 1], cs_["nm"], ALU.mult, ALU.add)
                    else:
                        stt(cs_["nm"], cs_["pm"], cs_["nst"][:, k:k + 1], cs_["nm"], ALU.mult, ALU.add)
            if seq == 0 and j == NCH - 1:
                dump("nm0", ch_s[0]["nm"], [128, 1])
            for i in range(4):
                b = 4 * j + i
                Nb = 128 * (b + 1)
                cs_ = ch_s[i]
                if mode[i] == "act":
                    ts("dve", Mb[:, 0:Nb], scores[i][:, 0:Nb], cs_["nm"], 0.0, ALU.add, ALU.is_ge)
                else:
                    ts("dve", Mb[:, 0:Nb], scores[i][:, 0:Nb], cs_["nm"], None, ALU.is_ge)
                for s0 in range(0, b + 1, 8):
                    n = min(8, b + 1 - s0)
                    pf, pb = psum.get()
                    for k in range(n):
                        tr(pb[:, k * 128:(k + 1) * 128], Mb[:, (s0 + k) * 128:(s0 + k + 1) * 128])
                    cp("act", MT[:, s0:s0 + n, :], pb[:, 0:n * 128].re("p (s t) -> p s t", s=n))
                po, _ = psB.get()
                units = [(h, s0) for h in range(6) for s0 in range(0, b + 1, 4)]

                def qk(u):
                    h, s0 = u
                    g, bk = h // 3, h % 3
                    n = min(4, b + 1 - s0)
                    pl, _ = psum.get()
                    for k in range(n):
                        st = s0 + k
                        mm(pl[:, k * 128:(k + 1) * 128], kT[g * 64:(g + 1) * 64, st * 128:(st + 1) * 128],
                           qT[g * 64:(g + 1) * 64, bk, i * 128:(i + 1) * 128], True, True)
                    return pl

                LA = 2
                pend = [qk(units[k_]) for k_ in range(min(LA, len(units)))]
                for ui, u in enumerate(units):
                    pl = pend.pop(0)
                    if ui + LA < len(units):
                        pend.append(qk(units[ui + LA]))
                    h, s0 = u
                    g = h // 3
                    n = min(4, b + 1 - s0)
                    et = ebuf.get()
                    act(et[:, 0:n * 128], pl[:, 0:n * 128], AF.Exp)
                    etm = ebuf.get()
                    tt("pool" if ui % 2 == 0 else "dve", etm[:, 0:n * 128].re("p (s t) -> p s t", s=n),
                       et[:, 0:n * 128].re("p (s t) -> p s t", s=n), MT[:, s0:s0 + n, :], ALU.mult)
                    for k in range(n):
                        st = s0 + k
                        mm(po[:, h * 65:(h + 1) * 65], etm[:, k * 128:(k + 1) * 128], vtok[:, st, g, :],
                           st == 0, st == b)
                rden = rden_a.get()
                recip(rden[:, 0:6], po[:, 0:390].re("p (h c) -> p h c", c=65)[:, :, 64])
                for h in range(6):
                    col = (h % 3) * 128 + (h // 3) * 64
                    act(att_tok[:, col:col + 64], po[:, h * 65:h * 65 + 64], AF.Copy, scale=rden[:, h:h + 1])
                pf, pb = psum.get()
                for bk in range(3):
                    tr(pb[:, bk * 128:(bk + 1) * 128], att_tok[:, bk * 128:(bk + 1) * 128])
                cp("dve", attT[:, :, i * 128:(i + 1) * 128], pb[:, 0:384].re("p (k t) -> p k t", k=3))
            if seq == 0 and j == NCH - 1:
                dump("attT", attT, [128, 3, TC], BF16)

            pms = [psB.get()[0] for _ in range(4)]
            for h in range(4):
                bk, p0 = h // 2, (h % 2) * 64
                ems = []
                for mt in range(2):
                    pl, _ = psum.get()
                    mm(pl, kmT[p0:p0 + 64, bk, mt * 128:(mt + 1) * 128], qcT[p0:p0 + 64, bk, :], True, True)
                    em = ebuf.get()
                    act(em, pl, AF.Exp)
                    ems.append(em)
                for tix in range(4):
                    for mt in range(2):
                        mm(pms[tix][:, h * 65:(h + 1) * 65], ems[mt][:, tix * 128:(tix + 1) * 128],
                           vmtok[:, mt, h, :], mt == 0, mt == 1)
            for tix in range(4):
                rden = rden_a.get()
                recip(rden[:, 0:4], pms[tix][:, 0:260].re("p (h c) -> p h c", c=65)[:, :, 64])
                for h in range(4):
                    act(mem_tok[:, h * 64:(h + 1) * 64], pms[tix][:, h * 65:h * 65 + 64], AF.Copy,
                        scale=rden[:, h:h + 1])
                pf, pb = psum.get()
                for bk in range(2):
                    tr(pb[:, bk * 128:(bk + 1) * 128], mem_tok[:, bk * 128:(bk + 1) * 128])
                cp("dve", memattT[:, :, tix * 128:(tix + 1) * 128], pb[:, 0:256].re("p (k t) -> p k t", k=2))
            if seq == 0 and j == 0:
                dump("memattT", memattT, [128, 2, TC], BF16)

            load(x1, x_d[seq, tok0:tok0 + TC, :].re("(t p) n -> p t n", p=128))
            for nb in range(8):
                wg = load_w(sG3[:, :, nb * 384:(nb + 1) * 384], 8, 384)
                pY = [psB.get()[0] for _ in range(3)]
                for kb in range(3):
                    mm(pY[0], wAO[:, kb, nb * 128:(nb + 1) * 128], attT[:, kb, :], kb == 0, kb == 2)
                for kb in range(3):
                    mm(pY[1], wCO[:, kb, nb * 128:(nb + 1) * 128], convT[:, kb, :], kb == 0, kb == 2)
                for kb in range(2):
                    mm(pY[2], wMO[:, kb, nb * 128:(nb + 1) * 128], memattT[:, kb, :], kb == 0, kb == 1)
                G = []
                for br in range(3):
                    pg, _ = psum.get()
                    for kb in range(8):
                        mm(pg, wg[:, kb, br * 128:(br + 1) * 128], hT[:, kb, :], kb == 0, kb == 7)
                    g_ = tmp.get()[:, 0:TC]
                    act(g_, pg, AF.Sigmoid, bias=bgate[:, nb * 3 + br:nb * 3 + br + 1])
                    G.append(g_)
                m = tmp.get()[:, 0:TC]
                t = tmp.get()[:, 0:TC]
                tt("dve", m, pY[0], G[0], ALU.mult)
                tt("dve", t, pY[1], G[1], ALU.mult)
                tt("pool", m, m, t, ALU.add)
                tt("dve", t, pY[2], G[2], ALU.mult)
                tt("pool", mergedT[:, nb, :], m, t, ALU.add)
            for n2 in range(2):
                wo = load_w(sO3[:, :, n2 * 512:(n2 + 1) * 512], 8, 512)
                for tix in range(4):
                    pf, _ = psum.get()
                    for kb in range(8):
                        mm(pf, mergedT[:, kb, tix * 128:(tix + 1) * 128], wo[:, kb, :], kb == 0, kb == 7)
                    xs = x1[:, tix, n2 * 512:(n2 + 1) * 512]
                    tt("dve", xs, pf, xs, ALU.add)
            if seq == 0 and j == 0:
                dump("x1", x1, [128, 4, D])

            if j + 1 < NCH:
                stage1(j + 1)
            rms_to_T_multi([x1[:, tix, :] for tix in range(4)], gffn_c, h2T, [0, 1, 2, 3])
            ffn_tail = [None]
            for grp in range(11):
                wu = load_w(sU3[:, :, grp * 512:(grp + 1) * 512], 8, 512)
                for pp in range(2):
                    p = grp * 2 + pp
                    pfs, ubs, accs, ubms = [], [], [], []
                    for which in range(2):
                        pf, _ = psum8.get()
                        for kb in range(8):
                            mm(pf, wu[:, kb, (pp * 2 + which) * 128:(pp * 2 + which + 1) * 128], h2T[:, kb, :],
                               kb == 0, kb == 7)
                        pfs.append(pf)
                    for which in range(2):
                        q = p * 2 + which
                        ub, ubh, ubm = ffn_ub.get()
                        cp("pool", ubh, fhalo[:, q, :])
                        ubs.append(ub)
                        ubms.append(ubm)
                    for which in range(2):
                        q = p * 2 + which
                        cp("act", ubms[which], pfs[which])
                        acc = ffn_acc.get()
                        act(acc, pfs[which], AF.Copy, scale=cwf[:, q * 3 + 2:q * 3 + 3])
                        accs.append(acc)
                    for which in range(2):
                        q = p * 2 + which
                        cp("pool", fhalo[:, q, :], ubms[which][:, TC - 2:TC])
                    for tap in (1, 0):
                        for which in range(2):
                            q = p * 2 + which
                            stt(accs[which], ubs[which][:, tap:tap + TC], cwf[:, q * 3 + tap:q * 3 + tap + 1],
                                accs[which], ALU.mult, ALU.add)
                    if ffn_tail[0] is not None:
                        ffn_tail[0]()

                    def tail(p=p, accs=accs):
                        sg = ffn_acc.get()
                        act(sg, accs[0], AF.Silu)
                        tt("pool", actT[:, p, :], sg, accs[1], ALU.mult)
                    ffn_tail[0] = tail
            ffn_tail[0]()
            ffn_tail[0] = None
            for n2 in range(2):
                pacc = [psB.get()[0] for _ in range(4)]
                for kg in range(0, 22, 8):
                    nk = min(8, 22 - kg)
                    wd = load_w(sD3[:, kg:kg + nk, n2 * 512:(n2 + 1) * 512], nk, 512)
                    for kk in range(nk):
                        kb = kg + kk
                        for tix in range(4):
                            mm(pacc[tix], actT[:, kb, tix * 128:(tix + 1) * 128], wd[:, kk, :], kb == 0, kb == 21)
                for tix in range(4):
                    xs = x1[:, tix, n2 * 512:(n2 + 1) * 512]
                    tt("dve", xs, pacc[tix], xs, ALU.add)
            rstd = rms_stats([x1[:, tix, :] for tix in range(4)])
            for tix in range(4):
                stt(x1[:, tix, :], x1[:, tix, :], rstd[:, tix:tix + 1], gfin, ALU.mult, ALU.mult)
            for tix in range(4):
                dst = out_d[seq, tok0 + tix * 128:tok0 + (tix + 1) * 128, :]
                C.dma("act", dst, x1[:, tix, :], x1.bufs[0], reads=[x1], writes=[dst])

    finals = [x1.bufs[0]] + [o.bufs[0] for o in obuf.items] + [d.bufs[0] for d in dbg_out.values()] + [out_d.bufs[0]]
    C.emit(finals)
    return nc, dbg_out


def _layout_weights(S, g_mix, w_in, b_gate, conv_w_short, w_att_out, w_conv_out, w_mem_out, w_o, g_mem,
                    w_mem_kv, g_ffn, w_up, conv_w_ffn, w_down, g_final):
    f = np.float32
    W = np.asarray(w_in[0], f)
    oq, ok, ov, oqi, oki, owi, oc, oqc, og = 0, 384, 512, 640, 896, 928, 936, 2088, 2344

    def partner(base, nheads, hd, half):
        idx = []
        for h in range(nheads):
            for d in range(hd):
                if d < half:
                    pd = d + half
                elif d < 2 * half:
                    pd = d - half
                else:
                    pd = d
                idx.append(base + h * hd + pd)
        return np.array(idx)

    cols = []
    qmain = [np.concatenate([np.arange(oq + i * 64, oq + (i + 1) * 64), np.arange(oq + (3 + i) * 64, oq + (4 + i) * 64)])
             for i in range(3)]
    qpart_all = partner(oq, 6, 64, 8)
    for i in range(3):
        cols.append(qmain[i])
        cols.append(qpart_all[qmain[i] - oq])
    cols.append(np.arange(ok, ok + 128))
    cols.append(partner(ok, 2, 64, 8))
    qip = partner(oqi, 8, 32, 4)
    for i in range(3):
        idx = np.arange(i * 96, i * 96 + 128) % 256
        cols.append(oqi + idx)
        cols.append(qip[idx])
    cols.append(np.tile(np.arange(oki, oki + 32), 4))
    cols.append(np.tile(partner(oki, 1, 32, 4), 4))
    for i in range(3):
        cols.append(np.arange(oc + 384 + i * 128, oc + 384 + (i + 1) * 128))
        cols.append(np.arange(oc + 768 + i * 128, oc + 768 + (i + 1) * 128))
        cols.append(np.arange(oc + i * 128, oc + (i + 1) * 128))
    cols.append(np.arange(oqc, oqc + 128))
    cols.append(np.arange(oqc + 128, oqc + 256))
    wA = np.ascontiguousarray(W[:, np.concatenate(cols)])
    wV = np.ascontiguousarray(W[:, np.concatenate([np.arange(ov, ov + 128), np.arange(owi, owi + 8)])])
    gcols = np.concatenate([np.arange(og + br * 1024 + nb * 128, og + br * 1024 + (nb + 1) * 128)
                            for nb in range(8) for br in range(3)])
    wG = np.ascontiguousarray(W[:, gcols])
    bg = np.asarray(b_gate[0], f)[gcols - og].reshape(24, 128).T
    arow = np.concatenate([np.concatenate([np.arange(i * 64, (i + 1) * 64), np.arange((3 + i) * 64, (4 + i) * 64)])
                           for i in range(3)])
    wAO = np.ascontiguousarray(np.asarray(w_att_out[0], f)[arow])
    ucols = np.concatenate([np.concatenate([np.arange(p * 128, (p + 1) * 128), np.arange(DFF + p * 128, DFF + (p + 1) * 128)])
                            for p in range(22)])
    wU = np.ascontiguousarray(np.asarray(w_up[0], f)[:, ucols])
    cwf = np.asarray(conv_w_ffn[0], f)[:, ucols]
    cwf = cwf.reshape(3, 44, 128).transpose(2, 1, 0).reshape(128, 132)
    cws = np.asarray(conv_w_short[0], f).reshape(3, 3, 128).transpose(2, 1, 0).reshape(128, 9)
    gcol = np.concatenate([np.asarray(g, f).reshape(8, 128).T for g in (g_mix[0], g_mem[0], g_ffn[0])], axis=1)
    gfin = np.broadcast_to(np.asarray(g_final, f)[None, :], (128, D))
    pos = np.arange(S, dtype=f)

    def tables(hd, half):
        inv = (ROPE_THETA ** (-(np.arange(half, dtype=f) / f(half)))).astype(f)
        ang = (pos[:, None] * inv[None, :]).astype(f)
        c, s = np.cos(ang).astype(f), np.sin(ang).astype(f)
        ct = np.ones((128, S), f)
        st = np.zeros((128, S), f)
        for p in range(128):
            d = p % hd
            if d < half:
                ct[p] = c[:, d]
                st[p] = -s[:, d]
            elif d < 2 * half:
                ct[p] = c[:, d - half]
                st[p] = s[:, d - half]
        return ct, st

    c64, s64 = tables(64, 8)
    c32, s32 = tables(32, 4)
    rope = np.stack([c64, s64, c32, s32])
    pw2 = np.broadcast_to((2.0 ** -(np.arange(KIT, dtype=f) + 2))[None, :], (128, KIT))
    c = np.ascontiguousarray
    return dict(wA=wA, wV=wV, wG=wG, wKV=c(np.asarray(w_mem_kv[0], f)), wAO=wAO,
                wCO=c(np.asarray(w_conv_out[0], f)), wMO=c(np.asarray(w_mem_out[0], f)),
                wO=c(np.asarray(w_o[0], f)), wU=wU, wD=c(np.asarray(w_down[0], f)),
                gcol=c(gcol), gfin=c(gfin), bgate=c(bg), cws=c(cws), cwf=c(cwf), rope=c(rope), pw2=c(pw2))


_NC_CACHE = {}


def run(x, mem, dbg=None, **wts):
    x = np.asarray(x, np.float32)
    mem = np.asarray(mem, np.float32)
    B, S, _ = x.shape
    NSEQ = B // NCORES
    key = (S, NSEQ, tuple(sorted(dbg)) if dbg else None)
    if key not in _NC_CACHE:
        _NC_CACHE[key] = build(S, NSEQ, dbg)
    nc, dbg_out = _NC_CACHE[key]
    shared = _layout_weights(S, **wts)
    in_maps = []
    for c in range(NCORES):
        m = dict(shared)
        m["x"] = np.ascontiguousarray(x[c * NSEQ:(c + 1) * NSEQ])
        m["mem"] = np.ascontiguousarray(mem[c * NSEQ:(c + 1) * NSEQ])
        in_maps.append(m)
    res = run_bass_kernel_spmd(nc, in_maps, core_ids=list(range(NCORES)))
    out = np.concatenate([np.asarray(r["out"]) for r in res.results], axis=0).astype(np.float32)
    if dbg:
        return out, {k: np.asarray(res.results[0]["dbg_" + k]) for k in dbg_out}
    return out


def kernel(x, mem, g_mix, w_in, b_gate, conv_w_short, w_att_out, w_conv_out, w_mem_out, w_o, g_mem,
           w_mem_kv, g_ffn, w_up, conv_w_ffn, w_down, g_final):
    return run(x, mem, g_mix=g_mix, w_in=w_in, b_gate=b_gate, conv_w_short=conv_w_short, w_att_out=w_att_out,
               w_conv_out=w_conv_out, w_mem_out=w_mem_out, w_o=w_o, g_mem=g_mem, w_mem_kv=w_mem_kv, g_ffn=g_ffn,
               w_up=w_up, conv_w_ffn=conv_w_ffn, w_down=w_down, g_final=g_final)
```
